# Optimizing a Trainium2 kernel written in Bass

```python
import math
import jax
import jax.numpy as jnp
from jax import lax
import numpy as np

D_MODEL = 1024
BATCH = 4
SEQ = 8192
DEPTH = 2

RMS_EPS = 1e-6
GN_EPS = 1e-5
ROPE_THETA = 10000.0

RET_HEADS = D_MODEL // 256
RET_QK_DIM = 128
RET_V_DIM = 2 * RET_QK_DIM
RET_CHUNK = 128

MLA_HEADS = D_MODEL // 128
MLA_Q_LORA = 3 * D_MODEL // 8
MLA_KV_LORA = D_MODEL // 4
MLA_NOPE = 128
MLA_ROPE = 64
MLA_V = 128
ATTN_BLOCK = 128

S5_WIDTH = D_MODEL
S5_GROUP = 16
S5_GROUPS = S5_WIDTH // S5_GROUP
S5_STATE = 64

N_BRANCH = 3
BRANCH_WIDTH = D_MODEL
FFN_HIDDEN = ((8 * D_MODEL + 3 * 256 - 1) // (3 * 256)) * 256

IN_SPLITS = (
    RET_HEADS * RET_QK_DIM,
    RET_HEADS * RET_QK_DIM,
    RET_HEADS * RET_V_DIM,
    RET_HEADS * RET_V_DIM,
    MLA_Q_LORA,
    MLA_KV_LORA,
    MLA_ROPE,
    S5_WIDTH,
    N_BRANCH * D_MODEL,
)
IN_WIDTH = sum(IN_SPLITS)

kernel_name = "hybrid_retention_mla_s5_gated_encoder"


def _rmsnorm(x, g):
    xf = x.astype(jnp.float32)
    y = xf * lax.rsqrt(jnp.mean(xf * xf, axis=-1, keepdims=True) + RMS_EPS)
    return y.astype(x.dtype) * g


def _rope_tables(seq, dim):
    inv = 1.0 / (ROPE_THETA ** (jnp.arange(0, dim, 2, dtype=jnp.float32) / dim))
    ang = jnp.arange(seq, dtype=jnp.float32)[:, None] * inv[None, :]
    return jnp.cos(ang), jnp.sin(ang)


def _apply_rope(x, cos, sin):
    half = x.shape[-1] // 2
    x1, x2 = x[..., :half], x[..., half:]
    return jnp.concatenate([x1 * cos - x2 * sin, x2 * cos + x1 * sin], axis=-1).astype(x.dtype)


def _retention_dir(q, k, v, log_g, strict):
    b, h, s, dk = q.shape
    dv = v.shape[-1]
    c = RET_CHUNK
    n = s // c
    q = q.reshape(b, h, n, c, dk)
    k = k.reshape(b, h, n, c, dk)
    v = v.reshape(b, h, n, c, dv)
    pos = jnp.arange(c, dtype=jnp.float32)
    diff = pos[:, None] - pos[None, :]
    mask = (diff > 0) if strict else (diff >= 0)
    decay_in = jnp.where(mask, jnp.exp(jnp.where(mask, diff, 0.0)[None] * log_g[:, None, None]), 0.0)
    scores = jnp.einsum('bhnid,bhnjd->bhnij', q, k) * decay_in[None, :, None]
    inner = jnp.einsum('bhnij,bhnje->bhnie', scores, v)
    k_w = jnp.exp((c - 1 - pos)[None, :] * log_g[:, None])
    chunk_kv = jnp.einsum('bhncd,bhnce->nbhde', k * k_w[None, :, None, :, None], v)
    chunk_decay = jnp.exp(c * log_g)[None, :, None, None]

    def step(state, kv):
        return chunk_decay * state + kv, state

    init = jnp.zeros(chunk_kv.shape[1:], chunk_kv.dtype)
    _, states = lax.scan(step, init, chunk_kv)
    q_w = jnp.exp((pos + 1)[None, :] * log_g[:, None])
    cross = jnp.einsum('bhncd,nbhde->bhnce', q, states) * q_w[None, :, None, :, None]
    return (inner + cross).reshape(b, h, s, dv)


def _retention_branch(q_flat, k_flat, v_flat, g_flat, ret_decay, gn_g, cos, sin):
    b, s, _ = q_flat.shape
    q = q_flat.reshape(b, s, RET_HEADS, RET_QK_DIM).transpose(0, 2, 1, 3)
    k = k_flat.reshape(b, s, RET_HEADS, RET_QK_DIM).transpose(0, 2, 1, 3)
    v = v_flat.reshape(b, s, RET_HEADS, RET_V_DIM).transpose(0, 2, 1, 3)
    q = _apply_rope(q, cos, sin) * (RET_QK_DIM ** -0.5)
    k = _apply_rope(k, cos, sin)
    log_g = jax.nn.log_sigmoid(ret_decay.astype(jnp.float32))
    fwd = _retention_dir(q, k, v, log_g[0], strict=False)
    bwd = _retention_dir(q[:, :, ::-1], k[:, :, ::-1], v[:, :, ::-1], log_g[1], strict=True)[:, :, ::-1]
    y = (fwd + bwd).astype(jnp.float32)
    mu = jnp.mean(y, axis=-1, keepdims=True)
    var = jnp.mean(jnp.square(y - mu), axis=-1, keepdims=True)
    y = (y - mu) * lax.rsqrt(var + GN_EPS)
    y = y.transpose(0, 2, 1, 3).reshape(b, s, RET_HEADS * RET_V_DIM).astype(q_flat.dtype) * gn_g
    return jax.nn.silu(g_flat) * y


def _mla_branch(c_q, c_kv, k_rope_flat, q_norm_g, w_uq, kv_norm_g, w_ukv, cos, sin):
    b, s, _ = c_q.shape
    q = (_rmsnorm(c_q, q_norm_g) @ w_uq).reshape(b, s, MLA_HEADS, MLA_NOPE + MLA_ROPE).transpose(0, 2, 1, 3)
    scale = (MLA_NOPE + MLA_ROPE) ** -0.5
    q_nope = q[..., :MLA_NOPE] * scale
    q_rope = _apply_rope(q[..., MLA_NOPE:], cos, sin) * scale
    kv = (_rmsnorm(c_kv, kv_norm_g) @ w_ukv).reshape(b, s, MLA_HEADS, MLA_NOPE + MLA_V).transpose(0, 2, 1, 3)
    k_nope, v = kv[..., :MLA_NOPE], kv[..., MLA_NOPE:]
    k_rope = _apply_rope(k_rope_flat, cos, sin)
    nb = s // ATTN_BLOCK
    qn_blocks = jnp.moveaxis(q_nope.reshape(b, MLA_HEADS, nb, ATTN_BLOCK, MLA_NOPE), 2, 0)
    qr_blocks = jnp.moveaxis(q_rope.reshape(b, MLA_HEADS, nb, ATTN_BLOCK, MLA_ROPE), 2, 0)

    def attend(blk):
        qn, qr = blk
        sc = (jnp.einsum('bhqd,bhkd->bhqk', qn, k_nope)
              + jnp.einsum('bhqd,bkd->bhqk', qr, k_rope))
        p = jax.nn.softmax(sc.astype(jnp.float32), axis=-1)
        return jnp.einsum('bhqk,bhkd->bhqd', p.astype(v.dtype), v)

    out = lax.map(attend, (qn_blocks, qr_blocks))
    return out.transpose(1, 0, 3, 2, 4).reshape(b, s, MLA_HEADS * MLA_V)


def _s5_direction(u_g, a_re, a_im, log_dt, b_re, b_im, c_re, c_im, reverse):
    dt = jnp.exp(log_dt)[:, None]
    ar = jnp.minimum(a_re, -1e-4)
    mag = jnp.exp(dt * ar)
    abar_re = mag * jnp.cos(dt * a_im)
    abar_im = mag * jnp.sin(dt * a_im)
    den = ar * ar + a_im * a_im
    nr = abar_re - 1.0
    ni = abar_im
    coef_re = (nr * ar + ni * a_im) / den
    coef_im = (ni * ar - nr * a_im) / den
    bb_re = coef_re[..., None] * b_re - coef_im[..., None] * b_im
    bb_im = coef_re[..., None] * b_im + coef_im[..., None] * b_re
    bu_re = jnp.einsum('bsgc,gpc->bsgp', u_g, bb_re)
    bu_im = jnp.einsum('bsgc,gpc->bsgp', u_g, bb_im)
    s = u_g.shape[1]
    a_re_t = jnp.broadcast_to(abar_re, (1, s) + abar_re.shape)
    a_im_t = jnp.broadcast_to(abar_im, (1, s) + abar_im.shape)

    def combine(e1, e2):
        a1r, a1i, b1r, b1i = e1
        a2r, a2i, b2r, b2i = e2
        return (a2r * a1r - a2i * a1i,
                a2r * a1i + a2i * a1r,
                a2r * b1r - a2i * b1i + b2r,
                a2r * b1i + a2i * b1r + b2i)

    _, _, xr, xi = lax.associative_scan(combine, (a_re_t, a_im_t, bu_re, bu_im), reverse=reverse, axis=1)
    return jnp.einsum('bsgp,gcp->bsgc', xr, c_re) - jnp.einsum('bsgp,gcp->bsgc', xi, c_im)


def _s5_branch(u, a_re, a_im, log_dt, b_re, b_im, c_re, c_im, d, w_glu):
    b, s, w = u.shape
    u_g = u.reshape(b, s, S5_GROUPS, S5_GROUP)
    y = (_s5_direction(u_g, a_re[0], a_im[0], log_dt[0], b_re[0], b_im[0], c_re[0], c_im[0], reverse=False)
         + _s5_direction(u_g, a_re[1], a_im[1], log_dt[1], b_re[1], b_im[1], c_re[1], c_im[1], reverse=True))
    y = y.reshape(b, s, w).astype(u.dtype) + d * u
    g = jax.nn.gelu(y)
    ga, gb = jnp.split(g @ w_glu, 2, axis=-1)
    return ga * jax.nn.sigmoid(gb)


def setup_inputs(seed: int = 0) -> dict:
    key = jax.random.key(seed)
    ks = jax.random.split(key, 24)
    f32 = jnp.float32
    L, D, G, P, C = DEPTH, D_MODEL, S5_GROUPS, S5_STATE, S5_GROUP

    def nrm(k, shape, scale):
        return jax.random.normal(k, shape, f32) * scale

    def gain(k, shape):
        return 1.0 + 0.02 * jax.random.normal(k, shape, f32)

    ret_logit = jnp.log(2.0 ** (5.0 + jnp.arange(RET_HEADS, dtype=f32)) - 1.0)
    ret_decay = ret_logit[None, None, :] + 0.05 * jax.random.normal(ks[3], (L, 2, RET_HEADS), f32)
    a_im_init = math.pi * jnp.arange(P, dtype=f32)

    return {
        "x": jax.random.normal(ks[0], (BATCH, SEQ, D), f32),
        "norm1_g": gain(ks[1], (L, D)),
        "w_in": nrm(ks[2], (L, D, IN_WIDTH), D ** -0.5),
        "ret_decay": ret_decay,
        "ret_gn_g": gain(ks[4], (L, RET_HEADS * RET_V_DIM)),
        "mla_q_norm_g": gain(ks[5], (L, MLA_Q_LORA)),
        "mla_w_uq": nrm(ks[6], (L, MLA_Q_LORA, MLA_HEADS * (MLA_NOPE + MLA_ROPE)), MLA_Q_LORA ** -0.5),
        "mla_kv_norm_g": gain(ks[7], (L, MLA_KV_LORA)),
        "mla_w_ukv": nrm(ks[8], (L, MLA_KV_LORA, MLA_HEADS * (MLA_NOPE + MLA_V)), MLA_KV_LORA ** -0.5),
        "s5_a_re": -0.5 + 0.01 * jax.random.normal(ks[9], (L, 2, G, P), f32),
        "s5_a_im": a_im_init + 0.01 * jax.random.normal(ks[10], (L, 2, G, P), f32),
        "s5_log_dt": jax.random.uniform(ks[11], (L, 2, G), f32, math.log(0.001), math.log(0.1)),
        "s5_b_re": nrm(ks[12], (L, 2, G, P, C), (2 * C) ** -0.5),
        "s5_b_im": nrm(ks[13], (L, 2, G, P, C), (2 * C) ** -0.5),
        "s5_c_re": nrm(ks[14], (L, 2, G, C, P), (2 * P) ** -0.5),
        "s5_c_im": nrm(ks[15], (L, 2, G, C, P), (2 * P) ** -0.5),
        "s5_d": nrm(ks[16], (L, S5_WIDTH), 1.0),
        "s5_w_glu": nrm(ks[17], (L, S5_WIDTH, 2 * S5_WIDTH), S5_WIDTH ** -0.5),
        "w_branch": nrm(ks[18], (L, N_BRANCH, BRANCH_WIDTH, D), BRANCH_WIDTH ** -0.5),
        "w_out": nrm(ks[19], (L, D, D), D ** -0.5),
        "norm2_g": gain(ks[20], (L, D)),
        "ffn_w_gu": nrm(ks[21], (L, D, 2 * FFN_HIDDEN), D ** -0.5),
        "ffn_w_down": nrm(ks[22], (L, FFN_HIDDEN, D), FFN_HIDDEN ** -0.5),
        "final_g": gain(ks[23], (D,)),
    }


def reference(x, norm1_g, w_in, ret_decay, ret_gn_g, mla_q_norm_g, mla_w_uq, mla_kv_norm_g, mla_w_ukv,
              s5_a_re, s5_a_im, s5_log_dt, s5_b_re, s5_b_im, s5_c_re, s5_c_im, s5_d, s5_w_glu,
              w_branch, w_out, norm2_g, ffn_w_gu, ffn_w_down, final_g):
    b, s, d = x.shape
    cos_r, sin_r = _rope_tables(s, RET_QK_DIM)
    cos_m, sin_m = _rope_tables(s, MLA_ROPE)
    split_at = list(np.cumsum(IN_SPLITS)[:-1])
    for l in range(DEPTH):
        h = _rmsnorm(x, norm1_g[l])
        (rq, rk, rv, rg, c_q, c_kv, k_rope, s5_u, gate_logits) = jnp.split(h @ w_in[l], split_at, axis=-1)
        y_ret = _retention_branch(rq, rk, rv, rg, ret_decay[l], ret_gn_g[l], cos_r, sin_r)
        y_mla = _mla_branch(c_q, c_kv, k_rope, mla_q_norm_g[l], mla_w_uq[l], mla_kv_norm_g[l], mla_w_ukv[l],
                            cos_m, sin_m)
        y_s5 = _s5_branch(s5_u, s5_a_re[l], s5_a_im[l], s5_log_dt[l], s5_b_re[l], s5_b_im[l],
                          s5_c_re[l], s5_c_im[l], s5_d[l], s5_w_glu[l])
        branches = jnp.stack([y_ret, y_mla, y_s5], axis=2)
        proj = jnp.einsum('bsnk,nkd->bsnd', branches, w_branch[l])
        gates = jax.nn.sigmoid(gate_logits.reshape(b, s, N_BRANCH, d))
        x = x + jnp.sum(gates * proj, axis=2) @ w_out[l]
        h2 = _rmsnorm(x, norm2_g[l])
        f_gate, f_up = jnp.split(h2 @ ffn_w_gu[l], 2, axis=-1)
        x = x + (jax.nn.silu(f_gate) * f_up) @ ffn_w_down[l]
    return _rmsnorm(x, final_g)
```

```python
import math
import os
from contextlib import ExitStack

import numpy as np
import ml_dtypes
import concourse.bass as bass
import concourse.mybir as mybir
from concourse.bass_utils import run_bass_kernel_spmd

F32 = mybir.dt.float32
BF16 = mybir.dt.bfloat16
AF = mybir.ActivationFunctionType
ALU = mybir.AluOpType

D = 1024
NH_R, DK_R, DV_R = 4, 128, 256
NH_M, QL, KVL, NOPE, ROPE, VD = 8, 384, 256, 128, 64, 128
G5, P5, C5 = 64, 64, 16
FF = 2816
INW = 7872
O_RQ, O_RK, O_RV, O_RG = 0, 512, 1024, 2048
O_CQ, O_CKV, O_KR, O_S5, O_GATE = 3072, 3456, 3712, 3776, 4800
RMS_EPS = 1e-6
GN_EPS = 1e-5
TT = 512


class Buf:
    __slots__ = ("w", "r", "excl")

    def __init__(self, excl=False):
        self.w = None
        self.r = {}
        self.excl = excl


class Sched:
    NDMA = 12

    def __init__(self, nc, es):
        self.nc = nc
        self.eng = {"pe": nc.tensor, "act": nc.scalar, "dve": nc.vector,
                    "pool": nc.gpsimd, "sp": nc.sync}
        self.semh = {}
        self.count = {}
        for e in self.eng:
            self.semh[e] = es.enter_context(nc.semaphore("c_" + e))
            self.count[e] = 0
        self.dma_val = {}
        self.dma_rr = {"sp": 0, "pool": 0, "act": 0}
        for q in ("sp", "pool", "act"):
            for i in range(self.NDMA):
                k = ("d", q, i)
                self.semh[k] = es.enter_context(nc.semaphore("d_%s%d" % (q, i)))
                self.dma_val[k] = 0
        self.waited = {e: {} for e in self.eng}
        self.n_inst = 0

    def wait(self, e, tk):
        if tk is None:
            return
        key, val = tk
        if key == "pe" and e == "pe":
            return
        if self.waited[e].get(key, 0) >= val:
            return
        self.eng[e].wait_ge(self.semh[key], val)
        self.waited[e][key] = val

    def _deps(self, reads, writes):
        deps = {}

        def add(tk):
            if tk is None:
                return
            k, v = tk
            if deps.get(k, 0) < v:
                deps[k] = v
        for b in reads:
            add(b.w)
            if b.excl:
                for k, v in b.r.items():
                    add((k, v))
        for b in writes:
            add(b.w)
            for k, v in b.r.items():
                add((k, v))
        return deps

    def _mark(self, tk, reads, writes):
        k, v = tk
        for b in reads:
            if b.r.get(k, 0) < v:
                b.r[k] = v
        for b in writes:
            b.w = tk
            b.r = {}

    def op(self, e, fn, reads=(), writes=()):
        for k, v in self._deps(reads, writes).items():
            self.wait(e, (k, v))
        inst = fn(self.eng[e])
        self.count[e] += 1
        inst.then_inc(self.semh[e], 1)
        tk = (e, self.count[e])
        self._mark(tk, reads, writes)
        self.n_inst += 1
        return tk

    def dma(self, q, out, in_, reads=(), writes=(), **kw):
        i = self.dma_rr[q]
        self.dma_rr[q] = (i + 1) % self.NDMA
        key = ("d", q, i)
        if self.dma_val[key]:
            self.wait(q, (key, self.dma_val[key]))
        for k, v in self._deps(reads, writes).items():
            self.wait(q, (k, v))
        inst = self.eng[q].dma_start(out=out, in_=in_, **kw)
        self.dma_val[key] += 16
        inst.then_inc(self.semh[key], 16)
        tk = (key, self.dma_val[key])
        self._mark(tk, reads, writes)
        self.n_inst += 1
        return tk

    def barrier(self):
        for e in self.eng:
            for e2 in self.eng:
                if e2 != e and self.count[e2]:
                    self.wait(e, (e2, self.count[e2]))
            for k, v in self.dma_val.items():
                if v:
                    self.wait(e, (k, v))


class Ctx:
    pass


def bufs(n):
    return [Buf() for _ in range(n)]


def pbufs(n):
    return [Buf(True) for _ in range(n)]


V_G1, V_G2, V_GN, V_S5D, V_QG, V_KVG, V_FIN = 0, 8, 16, 24, 32, 35, 37
NV = 45


_uid = [0]


def alloc_helpers(nc, ph):
    _uid[0] += 1
    u = _uid[0]

    def sb(name, shape, dt):
        return ph.enter_context(nc.sbuf_tensor("s%d_%s" % (u, name), shape, dt))

    def ps(name, shape, dt):
        return ph.enter_context(nc.psum_tensor("p%d_%s" % (u, name), shape, dt))
    return sb, ps


def load_consts(S, c, sb):
    nc = S.nc
    k = Ctx()
    k.b = Buf()
    k.identb = sb("identb", [128, 128], BF16)
    k.identf = sb("identf", [128, 128], F32)
    k.onesb = sb("onesb", [128, 128], BF16)
    k.p128 = sb("p128", [128, 128], BF16)
    k.p64 = sb("p64", [64, 64], BF16)
    S.dma("pool", k.identb[:], c.din["identf"][:, :], writes=[k.b])
    S.dma("sp", k.identf[:], c.din["identf"][:, :], writes=[k.b])
    S.dma("pool", k.onesb[:], c.din["onesf"][:, :], writes=[k.b])
    S.dma("pool", k.p128[:], c.din["p128f"][:, :], writes=[k.b])
    S.dma("pool", k.p64[:], c.din["p64f"][:, :], writes=[k.b])
    return k


def rope_epi(S, R, ps_ap, ps_b, cos_ap, sin_ap, tab_b, perm, kb, qs, qs_b, pq, pq_b,
             t1, t1_b, t2, t2_b, out_ap, out_b):
    S.op("act", lambda e: e.activation(out=qs[0:R, :], in_=ps_ap, func=AF.Copy),
         reads=[ps_b], writes=[qs_b])
    S.op("pe", lambda e: e.matmul(out=pq[0:R, :], lhsT=perm, rhs=qs[0:R, :], start=True, stop=True),
         reads=[qs_b, kb], writes=[pq_b])
    S.op("dve", lambda e: e.tensor_tensor(out=t1[0:R, :], in0=ps_ap, in1=cos_ap, op=ALU.mult),
         reads=[ps_b, tab_b], writes=[t1_b])
    S.op("dve", lambda e: e.tensor_tensor(out=t2[0:R, :], in0=pq[0:R, :], in1=sin_ap, op=ALU.mult),
         reads=[pq_b, tab_b], writes=[t2_b])
    S.op(os.environ.get("ROPE_ADD", "pool"), lambda e: e.tensor_tensor(out=out_ap, in0=t1[0:R, :], in1=t2[0:R, :], op=ALU.add),
         reads=[t1_b, t2_b], writes=[out_b])


def phase_A(S, c, l, K):
    nc = S.nc
    TM = c.TM
    NT = TM // TT
    NB = TM // 128
    dr = c.dr
    w_in = c.din["w_in"]
    with ExitStack() as ph:
        sb, ps = alloc_helpers(nc, ph)
        vec = sb("vecA", [128, NV], F32)
        vec_b = Buf()
        S.dma("sp", vec[:], c.din["vec"][l], writes=[vec_b])
        hT = sb("hT", [128, 8, TM], BF16)
        hT_b = bufs(NT)
        xin = [sb("xin%d" % i, [128, D], F32) for i in range(2)]
        xin_b = bufs(2)
        junk = sb("junk", [128, D], BF16)
        junk_b = Buf()
        hn = [sb("hn%d" % i, [128, D], BF16) for i in range(2)]
        hn_b = bufs(2)
        st = [sb("st%d" % i, [128, 4], F32) for i in range(2)]
        st_b = bufs(2)
        xts = [sb("xts%d" % i, [128, 8, 128], F32) for i in range(2)]
        xts_b = bufs(2)
        pT = [ps("pT%d" % i, [128, 8, 128], BF16) for i in range(1)]
        pT_b = pbufs(1)
        pX = [ps("pX%d" % i, [128, 8, 128], F32) for i in range(1)]
        pX_b = pbufs(1)
        acc = [ps("acc%d" % i, [128, TT], F32) for i in range(3)]
        acc_b = pbufs(3)
        pq = [ps("pq%d" % i, [128, TT], F32) for i in range(2)]
        pq_b = pbufs(2)
        wb = [sb("wb%d" % i, [128, 8, TT], BF16) for i in range(2)]
        wb_b = bufs(2)
        ob = [sb("ob%d" % i, [128, TM], BF16) for i in range(2)]
        ob_b = [bufs(NT) for _ in range(2)]
        ot = [sb("ot%d" % i, [128, TT], BF16) for i in range(3)]
        ot_b = bufs(3)
        tab = [sb("tab%d" % i, [128, 4, TT], F32) for i in range(2)]
        tab_b = bufs(2)
        qs = [sb("qs%d" % i, [128, TT], BF16) for i in range(2)]
        qs_b = bufs(2)
        t1 = [sb("t1_%d" % i, [128, TT], F32) for i in range(2)]
        t1_b = bufs(2)
        t2 = [sb("t2_%d" % i, [128, TT], F32) for i in range(2)]
        t2_b = bufs(2)
        rr = {"acc": 0, "w": 0, "ob": 0, "ot": 0, "rp": 0}

        def nxt(k, n):
            v = rr[k]
            rr[k] = (v + 1) % n
            return v

        for rng in ("other", "mine"):
            toff = TM if rng == "other" else 0
            mine = rng == "mine"
            for b in range(NB):
                s = b % 2
                t0 = toff + b * 128
                S.dma("sp", xin[s][:], c.din["x_all"][t0:t0 + 128, :], writes=[xin_b[s]])
                S.op("act", lambda e: e.activation(out=junk[:], in_=xin[s][:], func=AF.Square,
                                                   accum_out=st[s][:, 0:1]),
                     reads=[xin_b[s]], writes=[junk_b, st_b[s]])
                S.op("dve", lambda e: e.tensor_scalar(out=st[s][:, 1:2], in0=st[s][:, 0:1],
                                                      scalar1=1.0 / D, scalar2=RMS_EPS,
                                                      op0=ALU.mult, op1=ALU.add),
                     reads=[st_b[s]], writes=[st_b[s]])
                S.op("act", lambda e: e.activation(out=st[s][:, 2:3], in_=st[s][:, 1:2], func=AF.Sqrt),
                     reads=[st_b[s]], writes=[st_b[s]])
                S.op("dve", lambda e: e.reciprocal(out=st[s][:, 3:4], in_=st[s][:, 2:3]),
                     reads=[st_b[s]], writes=[st_b[s]])
                S.op("act", lambda e: e.activation(out=hn[s][:], in_=xin[s][:], func=AF.Copy,
                                                   scale=st[s][:, 3:4]),
                     reads=[xin_b[s], st_b[s]], writes=[hn_b[s]])
                for kc in range(8):
                    S.op("pe", lambda e: e.transpose(out=pT[0][:, kc, :], in_=hn[s][:, kc * 128:(kc + 1) * 128],
                                                     identity=K.identb[:]),
                         reads=[hn_b[s], K.b], writes=[pT_b[0]])
                j = b // 4
                S.op("dve", lambda e: e.tensor_tensor(
                    out=hT[:, :, b * 128:(b + 1) * 128], in0=pT[0][:],
                    in1=vec[:, V_G1:V_G1 + 8].unsqueeze(2).to_broadcast([128, 8, 128]), op=ALU.mult),
                    reads=[pT_b[0], vec_b], writes=[hT_b[j]])
                if mine:
                    for kc in range(8):
                        S.op("pe", lambda e: e.transpose(out=pX[0][:, kc, :],
                                                         in_=xin[s][:, kc * 128:(kc + 1) * 128],
                                                         identity=K.identf[:]),
                             reads=[xin_b[s], K.b], writes=[pX_b[0]])
                    S.op("act", lambda e: e.activation(out=xts[s][:], in_=pX[0][:], func=AF.Copy),
                         reads=[pX_b[0]], writes=[xts_b[s]])
                    S.dma("pool", dr["xT"][:, :, b * 128:(b + 1) * 128], xts[s][:], reads=[xts_b[s]])

            if os.environ.get("KDBG") == "A0":
                continue
            def load_w(c0, ncols):
                s = nxt("w", 2)
                S.dma("pool", wb[s][:, :, 0:ncols],
                      w_in[l].rearrange("(kc p) n -> p kc n", p=128)[:, :, c0:c0 + ncols],
                      writes=[wb_b[s]])
                return s

            def mm(a, ws, m0, m1, j):
                for kc in range(8):
                    S.op("pe", lambda e: e.matmul(out=acc[a][0:m1 - m0, :], lhsT=wb[ws][:, kc, m0:m1],
                                                  rhs=hT[:, kc, j * TT:(j + 1) * TT],
                                                  start=(kc == 0), stop=(kc == 7)),
                         reads=[wb_b[ws], hT_b[j]], writes=[acc_b[a]])

            def fm_group(c0, ncols, dst, ch0, func, eng="act"):
                ws = load_w(c0, ncols)
                nch = (ncols + 127) // 128
                for i in range(nch):
                    m0, m1 = i * 128, min(ncols, (i + 1) * 128)
                    R = m1 - m0
                    o = nxt("ob", 2)
                    for j in range(NT):
                        a = nxt("acc", 3)
                        mm(a, ws, m0, m1, j)
                        if func is None:
                            if (i + j) % 2 == 0:
                                S.op("dve", lambda e: e.tensor_copy(out=ob[o][0:R, j * TT:(j + 1) * TT],
                                                                    in_=acc[a][0:R, :]),
                                     reads=[acc_b[a]], writes=[ob_b[o][j]])
                            else:
                                S.op("act", lambda e: e.activation(out=ob[o][0:R, j * TT:(j + 1) * TT],
                                                                   in_=acc[a][0:R, :], func=AF.Copy),
                                     reads=[acc_b[a]], writes=[ob_b[o][j]])
                        else:
                            S.op("act", lambda e: e.activation(out=ob[o][0:R, j * TT:(j + 1) * TT],
                                                               in_=acc[a][0:R, :], func=func),
                                 reads=[acc_b[a]], writes=[ob_b[o][j]])
                    S.dma("sp", dst[0:R, ch0 + i, toff:toff + TM], ob[o][0:R, :], reads=ob_b[o])

            ws_q = load_w(O_RQ, 512) if mine else None
            ws_k = load_w(O_RK, 512)
            if os.environ.get("KDBG") == "A1w":
                continue
            for j in range(NT):
                ts = nxt("rp", 2)
                tg = toff + j * TT
                S.dma("sp", tab[ts][:], c.din["rtab"][:, :, tg:tg + TT], writes=[tab_b[ts]])
                for (ws, dst, ti) in ((ws_q, dr["qrT"], 0), (ws_k, dr["krT"], 2)):
                    if ws is None:
                        continue
                    for h in range(NH_R):
                        a = nxt("acc", 3)
                        mm(a, ws, h * 128, (h + 1) * 128, j)
                        if os.environ.get("KDBG") == "A1m":
                            continue
                        o = nxt("ot", 3)
                        p = (h % 2)
                        rope_epi(S, 128, acc[a][:], acc_b[a], tab[ts][:, ti, :], tab[ts][:, ti + 1, :],
                                 tab_b[ts], K.p128[:], K.b, qs[p], qs_b[p], pq[p], pq_b[p],
                                 t1[p], t1_b[p], t2[p], t2_b[p], ot[o][:], ot_b[o])
                        S.dma("sp", dst[h, :, tg:tg + TT], ot[o][:], reads=[ot_b[o]])
            if os.environ.get("KDBG") == "A1a":
                continue
            for g in range(2):
                ws = load_w(O_RV + g * 512, 512)
                for b in range(NB):
                    a = nxt("acc", 3)
                    for kc in range(8):
                        S.op("pe", lambda e: e.matmul(out=acc[a][:], lhsT=hT[:, kc, b * 128:(b + 1) * 128],
                                                      rhs=wb[ws][:, kc, :], start=(kc == 0), stop=(kc == 7)),
                             reads=[wb_b[ws], hT_b[b // 4]], writes=[acc_b[a]])
                    o = nxt("ot", 3)
                    if b % 2 == 0:
                        S.op("dve", lambda e: e.tensor_copy(out=ot[o][:], in_=acc[a][:]),
                             reads=[acc_b[a]], writes=[ot_b[o]])
                    else:
                        S.op("act", lambda e: e.activation(out=ot[o][:], in_=acc[a][:], func=AF.Copy),
                             reads=[acc_b[a]], writes=[ot_b[o]])
                    S.dma("sp", dr["rv"][toff + b * 128:toff + (b + 1) * 128, g * 512:(g + 1) * 512],
                          ot[o][:], reads=[ot_b[o]])
            if os.environ.get("KDBG") == "A1b":
                continue
            fm_group(O_CKV, 256, dr["ckvT"], 0, None)
            fm_group(O_KR, 64, dr["krawT"], 0, None)
            for g in range(2):
                fm_group(O_S5 + g * 512, 512, dr["s5uT"], g * 4, None)
            if os.environ.get("KDBG") == "A1c":
                continue
            if mine:
                fm_group(O_CQ, 384, dr["cqT"], 0, None)
                for g in range(2):
                    fm_group(O_RG + g * 512, 512, dr["rgT"], g * 4, AF.Silu)
                for g in range(6):
                    fm_group(O_GATE + g * 512, 512, dr["gatesT"], g * 4, AF.Sigmoid)
        S.barrier()


def scratch_spec(TM):
    T2 = 2 * TM
    return {
        "xT": ([128, 8, TM], F32),
        "qrT": ([NH_R, 128, TM], BF16),
        "krT": ([NH_R, 128, T2], BF16),
        "rv": ([T2, D], BF16),
        "rgT": ([128, 8, TM], BF16),
        "cqT": ([128, 3, TM], BF16),
        "ckvT": ([128, 2, T2], BF16),
        "krawT": ([64, 1, T2], BF16),
        "s5uT": ([128, 8, T2], BF16),
        "gatesT": ([128, 24, TM], BF16),
        "qnT": ([NH_M, 128, TM], BF16),
        "qpT": ([NH_M, 64, TM], BF16),
        "knT": ([NH_M, 128, T2], BF16),
        "kpT": ([64, T2], BF16),
        "vtok": ([T2, NH_M * VD], BF16),
        "yretT": ([128, 8, TM], BF16),
        "ymlaT": ([128, 8, TM], BF16),
        "ys5T": ([128, 8, TM], BF16),
        "gs5T": ([128, 8, TM], BF16),
        "sbst": ([TM // 128, NH_R, 128, DV_R], BF16),
        "x1T": ([128, 8, TM], F32),
        "h2T": ([128, 8, TM], BF16),
        "actT": ([128, FF // 128, TM], BF16),
    }


def input_spec(TM, L):
    T2 = 2 * TM
    return {
        "x_all": [T2, D],
        "w_in": [L, D, INW],
        "vec": [L, 128, NV],
        "ret_decay": [L, 2, NH_R],
        "mla_w_uq": [L, QL, NH_M * (NOPE + ROPE)],
        "mla_w_ukv": [L, KVL, NH_M * (NOPE + VD)],
        "s5_a_re": [L, 2, G5, P5], "s5_a_im": [L, 2, G5, P5], "s5_log_dt": [L, 2, G5],
        "s5_b_re": [L, 2, G5, P5, C5], "s5_b_im": [L, 2, G5, P5, C5],
        "s5_c_re": [L, 2, G5, C5, P5], "s5_c_im": [L, 2, G5, C5, P5],
        "s5_w_glu": [L, D, 2 * D],
        "w_branch": [L, 3, D, D],
        "w_out": [L, D, D],
        "ffn_w_gu": [L, D, 2 * FF],
        "ffn_w_down": [L, FF, D],
        "rtab": [128, 4, T2],
        "mtab": [64, 4, T2],
        "identf": [128, 128], "onesf": [128, 128], "p128f": [128, 128], "p64f": [64, 64],
        "rmask": [128, 2, 128],
        "rcst": [128, 4, 128],
        "s5mask": [128, 9, 128],
    }


def build(TM, L, phases, debug=(), final=True):
    nc = bass.Bass("TRN2", target_bir_lowering=False)
    c = Ctx()
    c.TM = TM
    c.L = L
    c.din = {}
    for name, shp in input_spec(TM, L).items():
        c.din[name] = nc.dram_tensor(name, shp, F32, kind="ExternalInput").ap()
    c.dr = {}
    for name, (shp, dt) in scratch_spec(TM).items():
        if name in debug:
            c.dr[name] = nc.dram_tensor(name, shp, dt, kind="ExternalOutput").ap()
        else:
            c.dr[name] = nc.dram_tensor(name, shp, dt).ap()
    c.out = nc.dram_tensor("out", [TM, D], F32, kind="ExternalOutput").ap()
    with ExitStack() as es:
        S = Sched(nc, es)
        sbp, _ = alloc_helpers(nc, es)
        K = load_consts(S, c, sbp)
        for l in range(L):
            last = final and (l == L - 1)
            if "A" in phases:
                phase_A(S, c, l, K)
            if "M" in phases:
                phase_M(S, c, l, K)
            if "B" in phases:
                phase_B(S, c, l, K)
            if "R" in phases:
                phase_R(S, c, l, K)
            if "S" in phases:
                phase_S(S, c, l, K)
            if "E" in phases and "S" not in phases:
                with ExitStack() as ph:
                    sbz, _ = alloc_helpers(nc, ph)
                    zt = sbz("zt", [128, 8, TM], BF16)
                    zb = Buf()
                    S.op("dve", lambda e: e.memset(zt[:], 0.0), writes=[zb])
                    S.dma("sp", c.dr["gs5T"][:, :, :], zt[:], reads=[zb])
                    S.barrier()
            if "E" in phases:
                phase_E(S, c, l, K, last)
        S.barrier()
        c.n_inst = S.n_inst
    return nc, c


def _rope_tab(pos, dim, qscale):
    half = dim // 2
    inv = 1.0 / (10000.0 ** (np.arange(0, dim, 2, dtype=np.float32) / dim))
    ang = pos.astype(np.float32)[None, :] * inv[:, None].astype(np.float32)
    cos = np.cos(ang).astype(np.float32)
    sin = np.sin(ang).astype(np.float32)
    cosf = np.concatenate([cos, cos], 0)
    sinf = np.concatenate([-sin, sin], 0)
    return np.stack([cosf * np.float32(qscale), sinf * np.float32(qscale), cosf, sinf], 1).astype(np.float32)


def _fm(v, n):
    return np.ascontiguousarray(np.asarray(v, np.float32).reshape(n, 128).T)


def host_consts(TM, S_len, typ):
    T2 = 2 * TM
    t = np.arange(T2)
    pos = t if typ == 0 else (S_len - 1 - t)
    cst = {}
    cst["rtab"] = _rope_tab(pos, DK_R, DK_R ** -0.5)
    cst["mtab"] = _rope_tab(pos, ROPE, (NOPE + ROPE) ** -0.5)
    cst["identf"] = np.eye(128, dtype=np.float32)
    cst["onesf"] = np.ones((128, 128), np.float32)
    p = np.zeros((128, 128), np.float32)
    for m in range(128):
        p[(m + 64) % 128, m] = 1
    cst["p128f"] = p
    p = np.zeros((64, 64), np.float32)
    for m in range(64):
        p[(m + 32) % 64, m] = 1
    cst["p64f"] = p
    j = np.arange(128)[:, None]
    i = np.arange(128)[None, :]
    if typ == 0:
        mf = (i >= j)
        mb = (j > i)
    else:
        mf = (i > j)
        mb = (j >= i)
    cst["rmask"] = np.stack([mf, mb], 1).astype(np.float32)
    rc = np.zeros((128, 4, 128), np.float32)
    rc[:, 0, :] = (i - j)
    rc[:, 1, :] = (i + 1)
    rc[:, 2, :] = (128 - i)
    rc[:, 3, 0] = 127 - np.arange(128)
    rc[:, 3, 1] = np.arange(128)
    cst["rcst"] = rc
    sm = np.zeros((128, 9, 128), np.float32)
    part = np.arange(128)
    col = np.arange(128)
    g2p = part // 64
    g2c = (col // 16) % 2
    qqc = col // 32
    sm[:, 0, :] = (g2p[:, None] == g2c[None, :])
    qqr = part // 32
    for qq in range(4):
        sm[:, 1 + qq, :] = (qqr == qq)[:, None]
        sm[:, 5 + qq, :] = (g2p[:, None] == g2c[None, :]) & (qqc[None, :] == qq)
    cst["s5mask"] = sm
    return cst


def host_vec(inp, L):
    vec = np.zeros((L, 128, NV), np.float32)
    for l in range(L):
        vec[l, :, V_G1:V_G1 + 8] = _fm(inp["norm1_g"][l], 8)
        vec[l, :, V_G2:V_G2 + 8] = _fm(inp["norm2_g"][l], 8)
        vec[l, :, V_GN:V_GN + 8] = _fm(inp["ret_gn_g"][l], 8)
        vec[l, :, V_S5D:V_S5D + 8] = _fm(inp["s5_d"][l], 8)
        vec[l, :, V_QG:V_QG + 3] = _fm(inp["mla_q_norm_g"][l], 3)
        vec[l, :, V_KVG:V_KVG + 2] = _fm(inp["mla_kv_norm_g"][l], 2)
        vec[l, :, V_FIN:V_FIN + 8] = _fm(inp["final_g"], 8)
    return vec


DIR_KEYS = ("ret_decay", "s5_a_re", "s5_a_im", "s5_log_dt", "s5_b_re", "s5_b_im", "s5_c_re", "s5_c_im")
W_KEYS = ("w_in", "mla_w_uq", "mla_w_ukv", "s5_w_glu", "w_branch", "w_out", "ffn_w_gu", "ffn_w_down")


def core_maps(x, inp, layers, TM):
    B, S_len, _ = x.shape
    L = len(layers)
    sl = slice(layers[0], layers[-1] + 1)
    shared = {k: np.ascontiguousarray(np.asarray(inp[k], np.float32)[sl]) for k in W_KEYS}
    vec = host_vec(inp, inp["w_in"].shape[0])[sl]
    dirs = [{k: np.ascontiguousarray(np.asarray(inp[k], np.float32)[sl]) for k in DIR_KEYS},
            {k: np.ascontiguousarray(np.asarray(inp[k], np.float32)[sl][:, ::-1]) for k in DIR_KEYS}]
    csts = [host_consts(TM, S_len, 0), host_consts(TM, S_len, 1)]
    maps = []
    for b in range(B):
        for typ in range(2):
            xa = x[b] if typ == 0 else x[b, ::-1]
            m = {"x_all": np.ascontiguousarray(xa, dtype=np.float32), "vec": vec}
            m.update(shared)
            m.update(dirs[typ])
            m.update(csts[typ])
            maps.append(m)
    return maps


def load_w_cast(S, wsb, wsb_b, src, nk, ncols):
    for kc in range(nk):
        S.dma("pool", wsb[:, kc, 0:ncols], src[kc * 128:(kc + 1) * 128, 0:ncols], writes=[wsb_b])


def lat_norm(S, K, x_sb, x_b, nk, dim, gcol, vec, vec_b, sq, sq_b, ssp, ssp_b, rs, rs_b, out, out_b):
    S.op("act", lambda e: e.activation(out=sq[:, 0:nk, :], in_=x_sb[:, 0:nk, :], func=AF.Square),
         reads=[x_b], writes=[sq_b])
    for kc in range(nk):
        S.op("pe", lambda e: e.matmul(out=ssp[:], lhsT=K.onesb[:], rhs=sq[:, kc, :],
                                      start=(kc == 0), stop=(kc == nk - 1)),
             reads=[sq_b, K.b], writes=[ssp_b])
    S.op("dve", lambda e: e.tensor_scalar(out=rs[:], in0=ssp[:], scalar1=1.0 / dim, scalar2=RMS_EPS,
                                          op0=ALU.mult, op1=ALU.add), reads=[ssp_b], writes=[rs_b])
    S.op("act", lambda e: e.activation(out=rs[:], in_=rs[:], func=AF.Sqrt), reads=[rs_b], writes=[rs_b])
    S.op("dve", lambda e: e.reciprocal(out=rs[:], in_=rs[:]), reads=[rs_b], writes=[rs_b])
    for kc in range(nk):
        S.op("dve", lambda e: e.scalar_tensor_tensor(out=out[:, kc, :], in0=x_sb[:, kc, :],
                                                     scalar=vec[:, gcol + kc:gcol + kc + 1], in1=rs[:],
                                                     op0=ALU.mult, op1=ALU.mult),
             reads=[x_b, rs_b, vec_b], writes=[out_b])


def phase_M(S, c, l, K):
    nc = S.nc
    TM = c.TM
    NT = TM // TT
    dr = c.dr
    sc = (NOPE + ROPE) ** -0.5
    with ExitStack() as ph:
        sb, ps = alloc_helpers(nc, ph)
        vec = sb("vec", [128, NV], F32)
        vec_b = Buf()
        S.dma("sp", vec[:], c.din["vec"][l], writes=[vec_b])
        wuq = sb("wuq", [128, 3, NH_M * 192], BF16)
        wuq_b = Buf()
        wukv = sb("wukv", [128, 2, NH_M * 256], BF16)
        wukv_b = Buf()
        load_w_cast(S, wuq, wuq_b, c.din["mla_w_uq"][l], 3, NH_M * 192)
        load_w_cast(S, wukv, wukv_b, c.din["mla_w_ukv"][l], 2, NH_M * 256)
        xk = [sb("xk%d" % i, [128, 2, TT], BF16) for i in range(2)]
        xk_b = bufs(2)
        xq = [sb("xq%d" % i, [128, 3, TT], BF16) for i in range(2)]
        xq_b = bufs(2)
        kr = [sb("kr%d" % i, [64, TT], BF16) for i in range(2)]
        kr_b = bufs(2)
        tab = [sb("tab%d" % i, [64, 4, TT], F32) for i in range(2)]
        tab_b = bufs(2)
        sq = sb("sq", [128, 3, TT], BF16)
        sq_b = Buf()
        rs = sb("rs", [128, TT], F32)
        rs_b = Buf()
        nk_ = sb("nk", [128, 2, TT], BF16)
        nk_b = Buf()
        nq_ = sb("nq", [128, 3, TT], BF16)
        nq_b = Buf()
        ssp = ps("ssp", [128, TT], F32)
        ssp_b = Buf(True)
        acc = [ps("acc%d" % i, [128, TT], F32) for i in range(4)]
        acc_b = pbufs(4)
        pq = [ps("pq%d" % i, [128, TT], F32) for i in range(2)]
        pq_b = pbufs(2)
        ot = [sb("ot%d" % i, [128, TT], BF16) for i in range(4)]
        ot_b = bufs(4)
        qs = [sb("qs%d" % i, [128, TT], BF16) for i in range(2)]
        qs_b = bufs(2)
        t1 = [sb("t1_%d" % i, [128, TT], F32) for i in range(2)]
        t1_b = bufs(2)
        t2 = [sb("t2_%d" % i, [128, TT], F32) for i in range(2)]
        t2_b = bufs(2)
        rr = {"acc": 0, "ot": 0, "rp": 0}

        def nxt(k, n):
            v = rr[k]
            rr[k] = (v + 1) % n
            return v

        def evac(a, R, dst_ap, scale=None):
            o = nxt("ot", 4)
            if scale is None and (a % 2 == 0):
                S.op("dve", lambda e: e.tensor_copy(out=ot[o][0:R, :], in_=acc[a][0:R, :]),
                     reads=[acc_b[a]], writes=[ot_b[o]])
            else:
                S.op("act", lambda e: e.activation(out=ot[o][0:R, :], in_=acc[a][0:R, :], func=AF.Copy,
                                                   scale=(1.0 if scale is None else scale)),
                     reads=[acc_b[a]], writes=[ot_b[o]])
            S.dma("sp", dst_ap, ot[o][0:R, :], reads=[ot_b[o]])

        wv = [wukv[:, kc, :].rearrange("p (h e) -> p h e", e=256) for kc in range(2)]
        for j in range(2 * NT):
            s = j % 2
            tg = j * TT
            mine = j < NT
            S.dma("sp", xk[s][:], dr["ckvT"][:, :, tg:tg + TT], writes=[xk_b[s]])
            S.dma("sp", kr[s][:], dr["krawT"][:, 0, tg:tg + TT], writes=[kr_b[s]])
            S.dma("sp", tab[s][:], c.din["mtab"][:, :, tg:tg + TT], writes=[tab_b[s]])
            lat_norm(S, K, xk[s], xk_b[s], 2, KVL, V_KVG, vec, vec_b, sq, sq_b, ssp, ssp_b, rs, rs_b, nk_, nk_b)
            for h in range(NH_M):
                a = nxt("acc", 4)
                for kc in range(2):
                    S.op("pe", lambda e: e.matmul(out=acc[a][:], lhsT=wukv[:, kc, 256 * h:256 * h + 128],
                                                  rhs=nk_[:, kc, :], start=(kc == 0), stop=(kc == 1)),
                         reads=[wukv_b, nk_b], writes=[acc_b[a]])
                evac(a, 128, dr["knT"][h, :, tg:tg + TT])
            for blk in range(4):
                for half in range(2):
                    a = nxt("acc", 4)
                    for hh in range(4):
                        hd = 4 * half + hh
                        for kc in range(2):
                            S.op("pe", lambda e: e.matmul(out=acc[a][:, hh * 128:(hh + 1) * 128],
                                                          lhsT=nk_[:, kc, blk * 128:(blk + 1) * 128],
                                                          rhs=wukv[:, kc, 256 * hd + 128:256 * hd + 256],
                                                          start=(kc == 0), stop=(kc == 1), skip_group_check=True),
                                 reads=[wukv_b, nk_b], writes=[acc_b[a]])
                    evac(a, 128, dr["vtok"][tg + blk * 128:tg + (blk + 1) * 128, half * 512:(half + 1) * 512])
            p = 0
            o = nxt("ot", 4)
            S.op("pe", lambda e: e.matmul(out=pq[p][0:64, :], lhsT=K.p64[:], rhs=kr[s][:], start=True, stop=True),
                 reads=[kr_b[s], K.b], writes=[pq_b[p]])
            S.op("dve", lambda e: e.tensor_tensor(out=t1[p][0:64, :], in0=kr[s][:], in1=tab[s][:, 2, :], op=ALU.mult),
                 reads=[kr_b[s], tab_b[s]], writes=[t1_b[p]])
            S.op("dve", lambda e: e.tensor_tensor(out=t2[p][0:64, :], in0=pq[p][0:64, :], in1=tab[s][:, 3, :], op=ALU.mult),
                 reads=[pq_b[p], tab_b[s]], writes=[t2_b[p]])
            S.op("pool", lambda e: e.tensor_tensor(out=ot[o][0:64, :], in0=t1[p][0:64, :], in1=t2[p][0:64, :], op=ALU.add),
                 reads=[t1_b[p], t2_b[p]], writes=[ot_b[o]])
            S.dma("sp", dr["kpT"][:, tg:tg + TT], ot[o][0:64, :], reads=[ot_b[o]])
            if mine:
                S.dma("sp", xq[s][:], dr["cqT"][:, :, tg:tg + TT], writes=[xq_b[s]])
                lat_norm(S, K, xq[s], xq_b[s], 3, QL, V_QG, vec, vec_b, sq, sq_b, ssp, ssp_b, rs, rs_b, nq_, nq_b)
                for h in range(NH_M):
                    a = nxt("acc", 4)
                    for kc in range(3):
                        S.op("pe", lambda e: e.matmul(out=acc[a][:], lhsT=wuq[:, kc, 192 * h:192 * h + 128],
                                                      rhs=nq_[:, kc, :], start=(kc == 0), stop=(kc == 2)),
                             reads=[wuq_b, nq_b], writes=[acc_b[a]])
                    evac(a, 128, dr["qnT"][h, :, tg:tg + TT], scale=sc)
                    a = nxt("acc", 4)
                    for kc in range(3):
                        S.op("pe", lambda e: e.matmul(out=acc[a][0:64, :], lhsT=wuq[:, kc, 192 * h + 128:192 * h + 192],
                                                      rhs=nq_[:, kc, :], start=(kc == 0), stop=(kc == 2)),
                             reads=[wuq_b, nq_b], writes=[acc_b[a]])
                    o = nxt("ot", 4)
                    p = h % 2
                    rope_epi(S, 64, acc[a][0:64, :], acc_b[a], tab[s][:, 0, :], tab[s][:, 1, :], tab_b[s],
                             K.p64[:], K.b, qs[p], qs_b[p], pq[p], pq_b[p], t1[p], t1_b[p], t2[p], t2_b[p],
                             ot[o][0:64, :], ot_b[o])
                    S.dma("sp", dr["qpT"][h, :, tg:tg + TT], ot[o][0:64, :], reads=[ot_b[o]])
        S.barrier()


def phase_B(S, c, l, K):
    nc = S.nc
    TM = c.TM
    NT = TM // TT
    T2 = 2 * TM
    NKB = T2 // 128
    dr = c.dr
    with ExitStack() as ph:
        sb, ps = alloc_helpers(nc, ph)
        kp = sb("kp", [64, T2], BF16)
        kp_b = Buf()
        S.dma("sp", kp[:], dr["kpT"][:, :], writes=[kp_b])
        kn = [sb("kn%d" % i, [128, T2], BF16) for i in range(2)]
        kn_b = bufs(2)
        vv = [sb("vv%d" % i, [128, NKB, 128], BF16) for i in range(2)]
        vv_b = bufs(2)
        qn = [sb("qn%d" % i, [128, TM], BF16) for i in range(2)]
        qn_b = bufs(2)
        qp = [sb("qp%d" % i, [64, TM], BF16) for i in range(2)]
        qp_b = bufs(2)
        NS = 3
        sps = [ps("sps%d" % i, [128, TT], F32) for i in range(NS)]
        sps_b = pbufs(NS)
        pt = [sb("pt%d" % i, [128, TT], BF16) for i in range(NS)]
        pt_b = bufs(NS)
        ops = [ps("ops%d" % i, [128, TT], F32) for i in range(2)]
        ops_b = pbufs(2)
        dps = [ps("dps%d" % i, [128, TT], F32) for i in range(2)]
        dps_b = pbufs(2)
        rden = [sb("rden%d" % i, [128, TT], F32) for i in range(2)]
        rden_b = bufs(2)
        yo = [sb("yo%d" % i, [128, TT], BF16) for i in range(2)]
        yo_b = bufs(2)
        it = 0
        for h in range(NH_M):
            s = h % 2
            S.dma("sp", kn[s][:], dr["knT"][h, :, :], writes=[kn_b[s]])
            S.dma("act", vv[s][:], dr["vtok"][:, h * 128:(h + 1) * 128].rearrange("(b p) e -> p b e", p=128),
                  writes=[vv_b[s]])
            S.dma("sp", qn[s][:], dr["qnT"][h, :, :], writes=[qn_b[s]])
            S.dma("sp", qp[s][:], dr["qpT"][h, :, :], writes=[qp_b[s]])
            for qi in range(NT):
                o = it % 2
                it += 1
                qsl = slice(qi * TT, (qi + 1) * TT)

                def score(kb):
                    a = kb % NS
                    S.op("pe", lambda e: e.matmul(out=sps[a][:], lhsT=kn[s][:, kb * 128:(kb + 1) * 128],
                                                  rhs=qn[s][:, qsl], start=True, stop=False),
                         reads=[kn_b[s], qn_b[s]], writes=[sps_b[a]])
                    S.op("pe", lambda e: e.matmul(out=sps[a][:], lhsT=kp[:, kb * 128:(kb + 1) * 128],
                                                  rhs=qp[s][:, qsl], start=False, stop=True),
                         reads=[kp_b, qp_b[s]], writes=[sps_b[a]])
                    S.op("act", lambda e: e.activation(out=pt[a][:], in_=sps[a][:], func=AF.Exp),
                         reads=[sps_b[a]], writes=[pt_b[a]])

                score(0)
                score(1)
                for kb in range(NKB):
                    if kb + 2 < NKB:
                        score(kb + 2)
                    a = kb % NS
                    S.op("pe", lambda e: e.matmul(out=ops[o][:], lhsT=vv[s][:, kb, :], rhs=pt[a][:],
                                                  start=(kb == 0), stop=(kb == NKB - 1)),
                         reads=[vv_b[s], pt_b[a]], writes=[ops_b[o]])
                    S.op("pe", lambda e: e.matmul(out=dps[o][:], lhsT=K.onesb[:], rhs=pt[a][:],
                                                  start=(kb == 0), stop=(kb == NKB - 1)),
                         reads=[K.b, pt_b[a]], writes=[dps_b[o]])
                S.op("dve", lambda e: e.reciprocal(out=rden[o][:], in_=dps[o][:]), reads=[dps_b[o]], writes=[rden_b[o]])
                S.op("dve", lambda e: e.tensor_tensor(out=yo[o][:], in0=ops[o][:], in1=rden[o][:], op=ALU.mult),
                     reads=[ops_b[o], rden_b[o]], writes=[yo_b[o]])
                S.dma("sp", dr["ymlaT"][:, h, qsl], yo[o][:], reads=[yo_b[o]])
        S.barrier()


def phase_R(S, c, l, K):
    nc = S.nc
    TM = c.TM
    NBk = TM // 128
    dr = c.dr
    C = 128
    with ExitStack() as ph:
        sb, ps = alloc_helpers(nc, ph)
        vec = sb("vec", [128, NV], F32)
        cst = sb("rcst", [128, 4, 128], F32)
        msk = sb("rmask", [128, 2, 128], F32)
        dec = sb("dec", [128, 8], F32)
        cb = Buf()
        S.dma("sp", vec[:], c.din["vec"][l], writes=[cb])
        S.dma("sp", cst[:], c.din["rcst"][:, :, :], writes=[cb])
        S.dma("sp", msk[:], c.din["rmask"][:, :, :], writes=[cb])
        S.dma("sp", dec[:], c.din["ret_decay"][l].rearrange("x h -> (x h)").partition_broadcast(128), writes=[cb])
        lg = sb("lg", [128, 8], F32)
        nl = sb("nl", [128, 8], F32)
        gC = sb("gC", [128, 8], F32)
        kw = sb("kw", [128, 8], F32)
        DT = sb("DT", [128, 4, 128], F32)
        tmpD = sb("tmpD", [128, 4, 128], F32)
        qw = sb("qw", [128, 2, 4, 128], F32)
        S.op("act", lambda e: e.activation(out=nl[:], in_=dec[:], func=AF.Exp, scale=-1.0), reads=[cb], writes=[cb])
        S.op("dve", lambda e: e.tensor_scalar(out=nl[:], in0=nl[:], scalar1=1.0, scalar2=None, op0=ALU.add),
             reads=[cb], writes=[cb])
        S.op("act", lambda e: e.activation(out=nl[:], in_=nl[:], func=AF.Ln), reads=[cb], writes=[cb])
        S.op("dve", lambda e: e.tensor_scalar(out=lg[:], in0=nl[:], scalar1=-1.0, scalar2=None, op0=ALU.mult),
             reads=[cb], writes=[cb])
        S.op("act", lambda e: e.activation(out=gC[:], in_=lg[:], func=AF.Exp, scale=float(C)), reads=[cb], writes=[cb])
        for h in range(4):
            S.op("act", lambda e: e.activation(out=DT[:, h, :], in_=cst[:, 0, :], func=AF.Exp, scale=lg[:, h:h + 1]),
                 reads=[cb], writes=[cb])
            S.op("act", lambda e: e.activation(out=tmpD[:, h, :], in_=cst[:, 0, :], func=AF.Exp, scale=nl[:, 4 + h:5 + h]),
                 reads=[cb], writes=[cb])
            S.op("act", lambda e: e.activation(out=qw[:, 0, h, :], in_=cst[:, 1, :], func=AF.Exp, scale=lg[:, h:h + 1]),
                 reads=[cb], writes=[cb])
            S.op("act", lambda e: e.activation(out=qw[:, 1, h, :], in_=cst[:, 2, :], func=AF.Exp, scale=lg[:, 4 + h:5 + h]),
                 reads=[cb], writes=[cb])
            S.op("act", lambda e: e.activation(out=kw[:, h:h + 1], in_=lg[:, h:h + 1], func=AF.Exp, scale=cst[:, 3, 0:1]),
                 reads=[cb], writes=[cb])
            S.op("act", lambda e: e.activation(out=kw[:, 4 + h:5 + h], in_=lg[:, 4 + h:5 + h], func=AF.Exp, scale=cst[:, 3, 1:2]),
                 reads=[cb], writes=[cb])
        S.op("dve", lambda e: e.tensor_tensor(out=DT[:], in0=DT[:], in1=msk[:, 0:1, :].to_broadcast([128, 4, 128]), op=ALU.mult),
             reads=[cb], writes=[cb])
        S.op("dve", lambda e: e.tensor_tensor(out=tmpD[:], in0=tmpD[:], in1=msk[:, 1:2, :].to_broadcast([128, 4, 128]), op=ALU.mult),
             reads=[cb], writes=[cb])
        S.op("dve", lambda e: e.tensor_tensor(out=DT[:], in0=DT[:], in1=tmpD[:], op=ALU.add), reads=[cb], writes=[cb])

        kk = [sb("kk%d" % i, [128, 4, 128], BF16) for i in range(2)]
        kk_b = bufs(2)
        qq = [sb("qq%d" % i, [128, 4, 128], BF16) for i in range(2)]
        qq_b = bufs(2)
        vv = [sb("vv%d" % i, [128, 1024], BF16) for i in range(2)]
        vv_b = bufs(2)
        sbn = [sb("sbn%d" % i, [128, 4, 256], BF16) for i in range(2)]
        sbn_b = bufs(2)
        rg = [sb("rg%d" % i, [128, 8, 128], BF16) for i in range(2)]
        rg_b = bufs(2)
        St = [sb("St%d" % i, [128, 4, 256], F32) for i in range(2)]
        St_b = bufs(2)
        Sfb = sb("Sfb", [128, 4, 256], BF16)
        Sfb_b = Buf()
        sbo = [sb("sbo%d" % i, [128, 4, 256], BF16) for i in range(2)]
        sbo_b = bufs(2)
        ktok = sb("ktok", [128, 4, 128], BF16)
        ktok_b = Buf()
        qf = sb("qf", [128, 2, 4, 128], BF16)
        qf_b = bufs(2)
        AT = sb("AT", [128, 4, 128], BF16)
        AT_b = Buf()
        sqj = sb("sqj", [128, 4, 256], F32)
        sqj_b = Buf()
        stt = sb("stt", [128, 8, 4], F32)
        stt_b = Buf()
        yn = sb("yn", [128, 4, 256], BF16)
        yn_b = Buf()
        yt = sb("yt", [128, 8, 128], F32)
        yt_b = Buf()
        yb = [sb("yb%d" % i, [128, 8, 128], BF16) for i in range(2)]
        yb_b = bufs(2)
        ptr = ps("ptr", [128, 4, 128], BF16)
        ptr_b = Buf(True)
        kvps = ps("kvps", [128, 4, 256], F32)
        kvps_b = Buf(True)
        sps = ps("sps", [128, 4, 128], F32)
        sps_b = Buf(True)
        ops = ps("ops", [128, 4, 256], F32)
        ops_b = Buf(True)
        ptT = ps("ptT", [128, 8, 128], BF16)
        ptT_b = Buf(True)
        for x in range(2):
            S.op("dve", lambda e: e.memset(St[x][:], 0.0), writes=[St_b[x]])
        S.op("pool", lambda e: e.memset(Sfb[:], 0.0), writes=[Sfb_b])

        def load_kv(n, s):
            S.dma("sp", kk[s][:], dr["krT"][:, :, n * 128:(n + 1) * 128].rearrange("h d t -> d h t"), writes=[kk_b[s]])
            S.dma("sp", vv[s][:], dr["rv"][n * 128:(n + 1) * 128, :], writes=[vv_b[s]])

        def state_update(x, s):
            for h in range(4):
                S.op("pe", lambda e: e.transpose(out=ptr[:, h, :], in_=kk[s][:, h, :], identity=K.identb[:]),
                     reads=[kk_b[s], K.b], writes=[ptr_b])
            S.op("dve", lambda e: e.tensor_tensor(out=ktok[:], in0=ptr[:],
                                                  in1=kw[:, 4 * x:4 * x + 4].unsqueeze(2).to_broadcast([128, 4, 128]),
                                                  op=ALU.mult), reads=[ptr_b, cb], writes=[ktok_b])
            for h in range(4):
                S.op("pe", lambda e: e.matmul(out=kvps[:, h, :], lhsT=ktok[:, h, :], rhs=vv[s][:, h * 256:(h + 1) * 256],
                                              start=True, stop=True), reads=[ktok_b, vv_b[s]], writes=[kvps_b])
            for h in range(4):
                S.op("dve", lambda e: e.scalar_tensor_tensor(out=St[x][:, h, :], in0=St[x][:, h, :],
                                                             scalar=gC[:, 4 * x + h:4 * x + h + 1], in1=kvps[:, h, :],
                                                             op0=ALU.mult, op1=ALU.add),
                     reads=[kvps_b, cb, St_b[x]], writes=[St_b[x]])

        for n in range(2 * NBk - 1, -1, -1):
            s = n % 2
            if n < NBk:
                S.op("act", lambda e: e.activation(out=sbo[s][:], in_=St[1][:], func=AF.Copy),
                     reads=[St_b[1]], writes=[sbo_b[s]])
                S.dma("sp", dr["sbst"][n].rearrange("h d e -> d h e"), sbo[s][:], reads=[sbo_b[s]])
            if n == 0:
                break
            load_kv(n, s)
            state_update(1, s)
        S.barrier()
        for n in range(NBk):
            s = n % 2
            tsl = slice(n * 128, (n + 1) * 128)
            load_kv(n, s)
            S.dma("sp", qq[s][:], dr["qrT"][:, :, tsl].rearrange("h d t -> d h t"), writes=[qq_b[s]])
            S.dma("act", sbn[s][:], dr["sbst"][n].rearrange("h d e -> d h e"), writes=[sbn_b[s]])
            S.dma("act", rg[s][:], dr["rgT"][:, :, tsl], writes=[rg_b[s]])
            S.op("dve", lambda e: e.tensor_tensor(out=qf[:, 0], in0=qq[s][:], in1=qw[:, 0], op=ALU.mult),
                 reads=[qq_b[s], cb], writes=[qf_b[0]])
            S.op("pool", lambda e: e.tensor_tensor(out=qf[:, 1], in0=qq[s][:], in1=qw[:, 1], op=ALU.mult),
                 reads=[qq_b[s], cb], writes=[qf_b[1]])
            for h in range(4):
                S.op("pe", lambda e: e.matmul(out=sps[:, h, :], lhsT=kk[s][:, h, :], rhs=qq[s][:, h, :], start=True, stop=True),
                     reads=[kk_b[s], qq_b[s]], writes=[sps_b])
            S.op("dve", lambda e: e.tensor_tensor(out=AT[:], in0=sps[:], in1=DT[:], op=ALU.mult),
                 reads=[sps_b, cb], writes=[AT_b])
            for h in range(4):
                vs = vv[s][:, h * 256:(h + 1) * 256]
                S.op("pe", lambda e: e.matmul(out=ops[:, h, :], lhsT=AT[:, h, :], rhs=vs, start=True, stop=False),
                     reads=[AT_b, vv_b[s]], writes=[ops_b])
                S.op("pe", lambda e: e.matmul(out=ops[:, h, :], lhsT=qf[:, 0, h, :], rhs=Sfb[:, h, :], start=False, stop=False),
                     reads=[qf_b[0], Sfb_b], writes=[ops_b])
                S.op("pe", lambda e: e.matmul(out=ops[:, h, :], lhsT=qf[:, 1, h, :], rhs=sbn[s][:, h, :], start=False, stop=True),
                     reads=[qf_b[1], sbn_b[s]], writes=[ops_b])
            S.op("dve", lambda e: e.tensor_reduce(out=stt[:, 0, :], in_=ops[:], axis=mybir.AxisListType.X, op=ALU.add),
                 reads=[ops_b], writes=[stt_b])
            S.op("act", lambda e: e.activation(out=sqj[:], in_=ops[:], func=AF.Square), reads=[ops_b], writes=[sqj_b])
            S.op("dve", lambda e: e.tensor_reduce(out=stt[:, 1, :], in_=sqj[:], axis=mybir.AxisListType.X, op=ALU.add),
                 reads=[sqj_b], writes=[stt_b])
            S.op("dve", lambda e: e.tensor_scalar(out=stt[:, 2, :], in0=stt[:, 0, :], scalar1=1.0 / DV_R, scalar2=None, op0=ALU.mult),
                 reads=[stt_b], writes=[stt_b])
            S.op("dve", lambda e: e.tensor_tensor(out=stt[:, 3, :], in0=stt[:, 2, :], in1=stt[:, 2, :], op=ALU.mult),
                 reads=[stt_b], writes=[stt_b])
            S.op("dve", lambda e: e.scalar_tensor_tensor(out=stt[:, 4, :], in0=stt[:, 1, :], scalar=1.0 / DV_R, in1=stt[:, 3, :],
                                                         op0=ALU.mult, op1=ALU.subtract), reads=[stt_b], writes=[stt_b])
            S.op("dve", lambda e: e.tensor_scalar(out=stt[:, 4, :], in0=stt[:, 4, :], scalar1=GN_EPS, scalar2=None, op0=ALU.add),
                 reads=[stt_b], writes=[stt_b])
            S.op("act", lambda e: e.activation(out=stt[:, 5, :], in_=stt[:, 4, :], func=AF.Sqrt), reads=[stt_b], writes=[stt_b])
            S.op("dve", lambda e: e.reciprocal(out=stt[:, 6, :], in_=stt[:, 5, :]), reads=[stt_b], writes=[stt_b])
            S.op("dve", lambda e: e.scalar_tensor_tensor(out=stt[:, 7, :], in0=stt[:, 2, :], scalar=-1.0, in1=stt[:, 6, :],
                                                         op0=ALU.mult, op1=ALU.mult), reads=[stt_b], writes=[stt_b])
            for h in range(4):
                S.op("act", lambda e: e.activation(out=yn[:, h, :], in_=ops[:, h, :], func=AF.Identity,
                                                   scale=stt[:, 6, h:h + 1], bias=stt[:, 7, h:h + 1]),
                     reads=[ops_b, stt_b], writes=[yn_b])
            for cch in range(8):
                S.op("pe", lambda e: e.transpose(out=ptT[:, cch, :], in_=yn[:, cch // 2, (cch % 2) * 128:(cch % 2 + 1) * 128],
                                                 identity=K.identb[:]), reads=[yn_b, K.b], writes=[ptT_b])
            S.op("dve", lambda e: e.tensor_tensor(out=yt[:], in0=ptT[:],
                                                  in1=vec[:, V_GN:V_GN + 8].unsqueeze(2).to_broadcast([128, 8, 128]), op=ALU.mult),
                 reads=[ptT_b, cb], writes=[yt_b])
            S.op("pool", lambda e: e.tensor_tensor(out=yb[s][:], in0=yt[:], in1=rg[s][:], op=ALU.mult),
                 reads=[yt_b, rg_b[s]], writes=[yb_b[s]])
            S.dma("sp", dr["yretT"][:, :, tsl], yb[s][:], reads=[yb_b[s]])
            if n < NBk - 1:
                state_update(0, s)
                S.op("act", lambda e: e.activation(out=Sfb[:], in_=St[0][:], func=AF.Copy), reads=[St_b[0]], writes=[Sfb_b])
        S.barrier()


def fm_rmsnorm(S, K, x, x_b, gcol, vec, cb, sq, sq_b, ssp, ssp_b, rs, rs_b, out, out_b, n):
    S.op("act", lambda e: e.activation(out=sq[:, :, 0:n], in_=x[:, :, 0:n], func=AF.Square), reads=[x_b], writes=[sq_b])
    for kc in range(8):
        S.op("pe", lambda e: e.matmul(out=ssp[:, 0:n], lhsT=K.onesb[:], rhs=sq[:, kc, 0:n], start=(kc == 0), stop=(kc == 7)),
             reads=[sq_b, K.b], writes=[ssp_b])
    S.op("dve", lambda e: e.tensor_scalar(out=rs[:, 0:n], in0=ssp[:, 0:n], scalar1=1.0 / D, scalar2=RMS_EPS,
                                          op0=ALU.mult, op1=ALU.add), reads=[ssp_b], writes=[rs_b])
    S.op("act", lambda e: e.activation(out=rs[:, 0:n], in_=rs[:, 0:n], func=AF.Sqrt), reads=[rs_b], writes=[rs_b])
    S.op("dve", lambda e: e.reciprocal(out=rs[:, 0:n], in_=rs[:, 0:n]), reads=[rs_b], writes=[rs_b])
    for kc in range(8):
        eng = "dve" if kc % 2 == 0 else "dve"
        S.op(eng, lambda e: e.scalar_tensor_tensor(out=out[:, kc, 0:n], in0=x[:, kc, 0:n], scalar=vec[:, gcol + kc:gcol + kc + 1],
                                                   in1=rs[:, 0:n], op0=ALU.mult, op1=ALU.mult),
             reads=[x_b, rs_b, cb], writes=[out_b])


def phase_E(S, c, l, K, last):
    nc = S.nc
    TM = c.TM
    dr = c.dr
    TE = 256
    with ExitStack() as ph:
        sb, ps = alloc_helpers(nc, ph)
        vec = sb("vec", [128, NV], F32)
        cb = Buf()
        S.dma("sp", vec[:], c.din["vec"][l], writes=[cb])
        wbr = sb("wbr", [128, 3, 8, D], BF16)
        wbr_b = Buf()
        wout = sb("wout", [128, 8, D], BF16)
        wout_b = Buf()
        for n in range(3):
            load_w_cast(S, wbr[:, n], wbr_b, c.din["w_branch"][l, n], 8, D)
        load_w_cast(S, wout, wout_b, c.din["w_out"][l], 8, D)
        yb = [sb("y%d" % i, [128, 8, TE], BF16) for i in range(2)]
        yb_b = bufs(2)
        gt = [sb("g%d" % i, [128, 8, TE], BF16) for i in range(2)]
        gt_b = bufs(2)
        xt = sb("xt", [128, 8, TE], F32)
        xt_b = Buf()
        mg = sb("mg", [128, 8, TE], F32)
        mg_b = Buf()
        tmp = [sb("tmp%d" % i, [128, TE], F32) for i in range(2)]
        tmp_b = bufs(2)
        mgb = sb("mgb", [128, 8, TE], BF16)
        mgb_b = Buf()
        x1 = sb("x1", [128, 8, TE], F32)
        x1_b = Buf()
        sq = sb("sq", [128, 8, TE], BF16)
        sq_b = Buf()
        rs = sb("rs", [128, TE], F32)
        rs_b = Buf()
        h2 = sb("h2", [128, 8, TE], BF16)
        h2_b = Buf()
        acc = [ps("acc%d" % i, [128, 512], F32) for i in range(4)]
        acc_b = pbufs(4)
        ssp = ps("ssp", [128, 512], F32)
        ssp_b = Buf(True)
        srcs = ("yretT", "ymlaT", "gs5T")
        ai = 0
        it = 0
        for j in range(TM // TE):
            tsl = slice(j * TE, (j + 1) * TE)
            S.dma("act", xt[:], dr["xT"][:, :, tsl], writes=[xt_b])
            for n in range(3):
                s = it % 2
                it += 1
                S.dma("sp", yb[s][:], dr[srcs[n]][:, :, tsl], writes=[yb_b[s]])
                S.dma("act", gt[s][:], dr["gatesT"][:, 8 * n:8 * n + 8, tsl], writes=[gt_b[s]])
                for m in range(8):
                    a = ai % 4
                    ai += 1
                    for kc in range(8):
                        S.op("pe", lambda e: e.matmul(out=acc[a][:, 0:TE], lhsT=wbr[:, n, kc, m * 128:(m + 1) * 128],
                                                      rhs=yb[s][:, kc, :], start=(kc == 0), stop=(kc == 7)),
                             reads=[wbr_b, yb_b[s]], writes=[acc_b[a]])
                    if n == 0:
                        S.op("dve", lambda e: e.tensor_tensor(out=mg[:, m, :], in0=acc[a][:, 0:TE], in1=gt[s][:, m, :], op=ALU.mult),
                             reads=[acc_b[a], gt_b[s]], writes=[mg_b])
                    else:
                        t = m % 2
                        S.op("dve", lambda e: e.tensor_tensor(out=tmp[t][:], in0=acc[a][:, 0:TE], in1=gt[s][:, m, :], op=ALU.mult),
                             reads=[acc_b[a], gt_b[s]], writes=[tmp_b[t]])
                        if n == 1:
                            S.op("pool", lambda e: e.tensor_tensor(out=mg[:, m, :], in0=mg[:, m, :], in1=tmp[t][:], op=ALU.add),
                                 reads=[tmp_b[t], mg_b], writes=[mg_b])
                        else:
                            S.op("pool", lambda e: e.tensor_tensor(out=mgb[:, m, :], in0=mg[:, m, :], in1=tmp[t][:], op=ALU.add),
                                 reads=[tmp_b[t], mg_b], writes=[mgb_b])
            for m in range(8):
                a = ai % 4
                ai += 1
                for kc in range(8):
                    S.op("pe", lambda e: e.matmul(out=acc[a][:, 0:TE], lhsT=wout[:, kc, m * 128:(m + 1) * 128],
                                                  rhs=mgb[:, kc, :], start=(kc == 0), stop=(kc == 7)),
                         reads=[wout_b, mgb_b], writes=[acc_b[a]])
                S.op("dve", lambda e: e.tensor_tensor(out=x1[:, m, :], in0=acc[a][:, 0:TE], in1=xt[:, m, :], op=ALU.add),
                     reads=[acc_b[a], xt_b], writes=[x1_b])
            S.dma("sp", dr["x1T"][:, :, tsl], x1[:], reads=[x1_b])
            fm_rmsnorm(S, K, x1, x1_b, V_G2, vec, cb, sq, sq_b, ssp, ssp_b, rs, rs_b, h2, h2_b, TE)
            S.dma("sp", dr["h2T"][:, :, tsl], h2[:], reads=[h2_b])
        S.barrier()
    NT = TM // TT
    NHC = FF // 128
    with ExitStack() as ph:
        sb, ps = alloc_helpers(nc, ph)
        h2T = sb("h2T", [128, 8, TM], BF16)
        h2T_b = bufs(NT)
        for j in range(NT):
            S.dma("sp" if j % 2 == 0 else "act", h2T[:, :, j * TT:(j + 1) * TT], dr["h2T"][:, :, j * TT:(j + 1) * TT], writes=[h2T_b[j]])
        wg = [sb("wg%d" % i, [128, 8, 256], BF16) for i in range(2)]
        wg_b = bufs(2)
        ob = [sb("ob%d" % i, [128, TM], BF16) for i in range(2)]
        ob_b = [bufs(NT) for _ in range(2)]
        sg = [sb("sg%d" % i, [128, TT], F32) for i in range(2)]
        sg_b = bufs(2)
        gp = [ps("gp%d" % i, [128, TT], F32) for i in range(3)]
        gp_b = pbufs(3)
        up = [ps("up%d" % i, [128, TT], F32) for i in range(3)]
        up_b = pbufs(3)
        wgu = c.din["ffn_w_gu"][l]
        it = 0
        for hc in range(NHC):
            s = hc % 2
            for kc in range(8):
                S.dma("pool", wg[s][:, kc, 0:128], wgu[kc * 128:(kc + 1) * 128, hc * 128:(hc + 1) * 128], writes=[wg_b[s]])
                S.dma("pool", wg[s][:, kc, 128:256], wgu[kc * 128:(kc + 1) * 128, FF + hc * 128:FF + (hc + 1) * 128], writes=[wg_b[s]])
            for j in range(NT):
                a = it % 3
                t = it % 2
                it += 1
                for kc in range(8):
                    S.op("pe", lambda e: e.matmul(out=gp[a][:], lhsT=wg[s][:, kc, 0:128], rhs=h2T[:, kc, j * TT:(j + 1) * TT],
                                                  start=(kc == 0), stop=(kc == 7)), reads=[wg_b[s], h2T_b[j]], writes=[gp_b[a]])
                for kc in range(8):
                    S.op("pe", lambda e: e.matmul(out=up[a][:], lhsT=wg[s][:, kc, 128:256], rhs=h2T[:, kc, j * TT:(j + 1) * TT],
                                                  start=(kc == 0), stop=(kc == 7)), reads=[wg_b[s], h2T_b[j]], writes=[up_b[a]])
                S.op("act", lambda e: e.activation(out=sg[t][:], in_=gp[a][:], func=AF.Silu), reads=[gp_b[a]], writes=[sg_b[t]])
                S.op("dve", lambda e: e.tensor_tensor(out=ob[s][:, j * TT:(j + 1) * TT], in0=up[a][:], in1=sg[t][:], op=ALU.mult),
                     reads=[up_b[a], sg_b[t]], writes=[ob_b[s][j]])
            S.dma("sp", dr["actT"][:, hc, :], ob[s][:], reads=ob_b[s])
        S.barrier()
    with ExitStack() as ph:
        sb, ps = alloc_helpers(nc, ph)
        vec = sb("vec", [128, NV], F32)
        cb = Buf()
        S.dma("sp", vec[:], c.din["vec"][l], writes=[cb])
        wd = sb("wd", [128, NHC, D], BF16)
        wd_b = Buf()
        load_w_cast(S, wd, wd_b, c.din["ffn_w_down"][l], NHC, D)
        at = [sb("at%d" % i, [128, NHC, TE], BF16) for i in range(2)]
        at_b = bufs(2)
        x1 = [sb("x1_%d" % i, [128, 8, TE], F32) for i in range(2)]
        x1_b = bufs(2)
        x2 = sb("x2", [128, 8, TE], F32)
        x2_b = Buf()
        sq = sb("sq", [128, 8, TE], BF16)
        sq_b = Buf()
        rs = sb("rs", [128, TE], F32)
        rs_b = Buf()
        xo = sb("xo", [128, 8, TE], F32)
        xo_b = Buf()
        tok = [sb("tok%d" % i, [128, D], F32) for i in range(2)]
        tok_b = bufs(2)
        acc = [ps("acc%d" % i, [128, 512], F32) for i in range(3)]
        acc_b = pbufs(3)
        ssp = ps("ssp", [128, 512], F32)
        ssp_b = Buf(True)
        ptk = [ps("ptk%d" % i, [128, D], F32) for i in range(1)]
        ptk_b = pbufs(1)
        ai = 0
        bi = 0
        for j in range(TM // TE):
            s = j % 2
            tsl = slice(j * TE, (j + 1) * TE)
            S.dma("sp", at[s][:], dr["actT"][:, :, tsl], writes=[at_b[s]])
            S.dma("act", x1[s][:], dr["x1T"][:, :, tsl], writes=[x1_b[s]])
            for m in range(8):
                a = ai % 3
                ai += 1
                for hc in range(NHC):
                    S.op("pe", lambda e: e.matmul(out=acc[a][:, 0:TE], lhsT=wd[:, hc, m * 128:(m + 1) * 128], rhs=at[s][:, hc, :],
                                                  start=(hc == 0), stop=(hc == NHC - 1)), reads=[wd_b, at_b[s]], writes=[acc_b[a]])
                S.op("dve", lambda e: e.tensor_tensor(out=x2[:, m, :], in0=acc[a][:, 0:TE], in1=x1[s][:, m, :], op=ALU.add),
                     reads=[acc_b[a], x1_b[s]], writes=[x2_b])
            if last:
                fm_rmsnorm(S, K, x2, x2_b, V_FIN, vec, cb, sq, sq_b, ssp, ssp_b, rs, rs_b, xo, xo_b, TE)
                src, src_b = xo, xo_b
            else:
                src, src_b = x2, x2_b
            for blk in range(TE // 128):
                t = bi % 2
                bi += 1
                for kc in range(8):
                    S.op("pe", lambda e: e.transpose(out=ptk[0][:, kc * 128:(kc + 1) * 128],
                                                     in_=src[:, kc, blk * 128:(blk + 1) * 128], identity=K.identf[:]),
                         reads=[src_b, K.b], writes=[ptk_b[0]])
                S.op("act", lambda e: e.activation(out=tok[t][:], in_=ptk[0][:], func=AF.Copy), reads=[ptk_b[0]], writes=[tok_b[t]])
                r0 = j * TE + blk * 128
                S.dma("sp", c.out[r0:r0 + 128, :], tok[t][:], reads=[tok_b[t]])
        S.barrier()


def phase_S(S, c, l, K):
    nc = S.nc
    TM = c.TM
    T2 = 2 * TM
    NC = TM // 8
    N2 = 2 * NC
    dr = c.dr
    TWO_PI = 2.0 * math.pi
    I32 = mybir.dt.int32
    with ExitStack() as ph:
        sb, ps = alloc_helpers(nc, ph)
        cb = Buf()
        vec = sb("vec", [128, NV], F32)
        S.dma("sp", vec[:], c.din["vec"][l], writes=[cb])
        mk = sb("mk", [128, 9, 128], F32)
        S.dma("sp", mk[:], c.din["s5mask"][:, :, :], writes=[cb])
        hpi = sb("hpi", [128, 1], F32)
        S.op("dve", lambda e: e.memset(hpi[:], math.pi / 2), writes=[cb])
        NSQ = 11
        Z = sb("Z", [128, 2, 2, 9 + NSQ, 32], F32)
        ZN = sb("ZN", [128, 2, NSQ, 32], F32)
        Bb = sb("Bb", [128, 2, 2, 32, 16], F32)
        Cq = sb("Cq", [128, 2, 2, 32, 16], F32)
        pre = ExitStack()
        sbp, psp = alloc_helpers(nc, pre)
        pin = sbp("pin", [32, 2, 3, 128], F32)
        S.op("dve", lambda e: e.memset(pin[:], 0.0), writes=[cb])
        ldt = sbp("ldt", [32, 2, 2], F32)
        for x in range(2):
            S.dma("sp", pin[:, x, 0, :], c.din["s5_a_re"][l, x].rearrange("(q g) p -> q (g p)", g=2), writes=[cb])
            S.dma("sp", pin[:, x, 1, :], c.din["s5_a_im"][l, x].rearrange("(q g) p -> q (g p)", g=2), writes=[cb])
            S.dma("sp", ldt[:, x, :], c.din["s5_log_dt"][l, x].rearrange("(q g) -> q g", g=2), writes=[cb])
            S.op("dve", lambda e: e.tensor_copy(out=pin[:, x, 2, :].rearrange("q (g p) -> q g p", g=2),
                                                in_=ldt[:, x, :].unsqueeze(2).to_broadcast([32, 2, 64])), reads=[cb], writes=[cb])
        pp = psp("pp", [128, 2, 3, 32], F32)
        pp_b = Buf(True)
        for x in range(2):
            for i in range(3):
                S.op("pe", lambda e: e.transpose(out=pp[:, x, i, :], in_=pin[:, x, i, :], identity=K.identf[0:32, 0:32]),
                     reads=[cb, K.b], writes=[pp_b])
        P = sbp("P", [128, 2, 24, 32], F32)
        AR, AI, DT_, TH, MAG, TI, S4, C4, S2, C2, S1, C1, ZR1, ZI1, DEN, NR, T0, T1, CR, CI = range(20)
        Ti = sbp("Ti", [128, 2, 32], I32)

        def tt(o, a, b, op):
            S.op("dve", lambda e: e.tensor_tensor(out=P[:, :, o, :], in0=P[:, :, a, :], in1=P[:, :, b, :], op=op), reads=[cb], writes=[cb])

        def ts(o, a, s1, s2, op0, op1=None):
            if op1 is None:
                S.op("dve", lambda e: e.tensor_scalar(out=P[:, :, o, :], in0=P[:, :, a, :], scalar1=s1, scalar2=None, op0=op0), reads=[cb], writes=[cb])
            else:
                S.op("dve", lambda e: e.tensor_scalar(out=P[:, :, o, :], in0=P[:, :, a, :], scalar1=s1, scalar2=s2, op0=op0, op1=op1), reads=[cb], writes=[cb])

        S.op("dve", lambda e: e.tensor_scalar(out=P[:, :, AR, :], in0=pp[:, :, 0, :], scalar1=-1e-4, scalar2=None, op0=ALU.min), reads=[pp_b], writes=[cb])
        S.op("dve", lambda e: e.tensor_copy(out=P[:, :, AI, :], in_=pp[:, :, 1, :]), reads=[pp_b], writes=[cb])
        S.op("act", lambda e: e.activation(out=P[:, :, DT_, :], in_=pp[:, :, 2, :], func=AF.Exp), reads=[pp_b], writes=[cb])
        tt(T0, DT_, AR, ALU.mult)
        S.op("act", lambda e: e.activation(out=P[:, :, MAG, :], in_=P[:, :, T0, :], func=AF.Exp), reads=[cb], writes=[cb])
        tt(TH, DT_, AI, ALU.mult)
        ts(T0, TH, 1.0 / TWO_PI, None, ALU.mult)
        S.op("dve", lambda e: e.tensor_copy(out=Ti[:], in_=P[:, :, T0, :]), reads=[cb], writes=[cb])
        S.op("dve", lambda e: e.tensor_copy(out=P[:, :, TI, :], in_=Ti[:]), reads=[cb], writes=[cb])
        S.op("dve", lambda e: e.scalar_tensor_tensor(out=P[:, :, T1, :], in0=P[:, :, TI, :], scalar=-TWO_PI, in1=P[:, :, TH, :],
                                                     op0=ALU.mult, op1=ALU.add), reads=[cb], writes=[cb])
        S.op("act", lambda e: e.activation(out=P[:, :, S4, :], in_=P[:, :, T1, :], func=AF.Sin, scale=0.25), reads=[cb], writes=[cb])
        S.op("act", lambda e: e.activation(out=P[:, :, C4, :], in_=P[:, :, T1, :], func=AF.Sin, scale=0.25, bias=hpi[:, 0:1]), reads=[cb], writes=[cb])

        def dbl(so, co, si, ci):
            tt(T0, si, ci, ALU.mult)
            ts(so, T0, 2.0, None, ALU.mult)
            tt(T0, si, si, ALU.mult)
            ts(co, T0, -2.0, 1.0, ALU.mult, ALU.add)
        dbl(S2, C2, S4, C4)
        dbl(S1, C1, S2, C2)
        tt(ZR1, MAG, C1, ALU.mult)
        tt(ZI1, MAG, S1, ALU.mult)
        tt(T0, AR, AR, ALU.mult)
        tt(T1, AI, AI, ALU.mult)
        tt(DEN, T0, T1, ALU.add)
        S.op("dve", lambda e: e.reciprocal(out=P[:, :, DEN, :], in_=P[:, :, DEN, :]), reads=[cb], writes=[cb])
        ts(NR, ZR1, -1.0, None, ALU.add)
        tt(T0, NR, AR, ALU.mult)
        tt(T1, ZI1, AI, ALU.mult)
        tt(T0, T0, T1, ALU.add)
        tt(CR, T0, DEN, ALU.mult)
        tt(T0, ZI1, AR, ALU.mult)
        tt(T1, NR, AI, ALU.mult)
        tt(T0, T0, T1, ALU.subtract)
        tt(CI, T0, DEN, ALU.mult)

        def cmul(o, a, b):
            S.op("dve", lambda e: e.tensor_tensor(out=P[:, :, T0, :], in0=Z[:, :, 0, a, :], in1=Z[:, :, 0, b, :], op=ALU.mult), reads=[cb], writes=[cb])
            S.op("dve", lambda e: e.tensor_tensor(out=P[:, :, T1, :], in0=Z[:, :, 1, a, :], in1=Z[:, :, 1, b, :], op=ALU.mult), reads=[cb], writes=[cb])
            S.op("dve", lambda e: e.tensor_tensor(out=P[:, :, DEN, :], in0=Z[:, :, 0, a, :], in1=Z[:, :, 1, b, :], op=ALU.mult), reads=[cb], writes=[cb])
            S.op("dve", lambda e: e.tensor_tensor(out=P[:, :, NR, :], in0=Z[:, :, 1, a, :], in1=Z[:, :, 0, b, :], op=ALU.mult), reads=[cb], writes=[cb])
            S.op("dve", lambda e: e.tensor_tensor(out=Z[:, :, 0, o, :], in0=P[:, :, T0, :], in1=P[:, :, T1, :], op=ALU.subtract), reads=[cb], writes=[cb])
            S.op("dve", lambda e: e.tensor_tensor(out=Z[:, :, 1, o, :], in0=P[:, :, DEN, :], in1=P[:, :, NR, :], op=ALU.add), reads=[cb], writes=[cb])
        S.op("dve", lambda e: e.memset(Z[:, :, 0, 0, :], 1.0), writes=[cb])
        S.op("dve", lambda e: e.memset(Z[:, :, 1, 0, :], 0.0), writes=[cb])
        S.op("dve", lambda e: e.tensor_copy(out=Z[:, :, 0, 1, :], in_=P[:, :, ZR1, :]), reads=[cb], writes=[cb])
        S.op("dve", lambda e: e.tensor_copy(out=Z[:, :, 1, 1, :], in_=P[:, :, ZI1, :]), reads=[cb], writes=[cb])
        for k in range(2, 9):
            cmul(k, k - 1, 1)
        for j in range(NSQ - 1):
            cmul(9 + j, 8 + j, 8 + j)
        S.op("dve", lambda e: e.tensor_scalar(out=ZN[:], in0=Z[:, :, 1, 8:8 + NSQ, :], scalar1=-1.0, scalar2=None, op0=ALU.mult), reads=[cb], writes=[cb])
        Bq = sbp("Bq", [128, 2, 2, 32, 16], F32)
        for x in range(2):
            S.dma("sp", Bq[:, x, 0], c.din["s5_b_re"][l, x].rearrange("(q g) p c -> (g p) q c", g=2), writes=[cb])
            S.dma("act", Bq[:, x, 1], c.din["s5_b_im"][l, x].rearrange("(q g) p c -> (g p) q c", g=2), writes=[cb])
        tb = sbp("tb", [128, 2, 32, 16], F32)

        def bc(row, x):
            return P[:, x, row, :].unsqueeze(2).to_broadcast([128, 32, 16])
        for x in range(2):
            S.op("dve", lambda e: e.tensor_tensor(out=Bb[:, x, 0], in0=Bq[:, x, 0], in1=bc(CR, x), op=ALU.mult), reads=[cb], writes=[cb])
            S.op("dve", lambda e: e.tensor_tensor(out=tb[:, 0], in0=Bq[:, x, 1], in1=bc(CI, x), op=ALU.mult), reads=[cb], writes=[cb])
            S.op("dve", lambda e: e.tensor_tensor(out=Bb[:, x, 0], in0=Bb[:, x, 0], in1=tb[:, 0], op=ALU.subtract), reads=[cb], writes=[cb])
            S.op("dve", lambda e: e.tensor_tensor(out=Bb[:, x, 1], in0=Bq[:, x, 1], in1=bc(CR, x), op=ALU.mult), reads=[cb], writes=[cb])
            S.op("dve", lambda e: e.tensor_tensor(out=tb[:, 1], in0=Bq[:, x, 0], in1=bc(CI, x), op=ALU.mult), reads=[cb], writes=[cb])
            S.op("dve", lambda e: e.tensor_tensor(out=Bb[:, x, 1], in0=Bb[:, x, 1], in1=tb[:, 1], op=ALU.add), reads=[cb], writes=[cb])
        Cn = sbp("Cn", [32, 16, 2, 64], F32)
        Cn_b = Buf()
        cp = psp("cp", [128, 16, 32], F32)
        cp_b = Buf(True)
        for x in range(2):
            for ri, nm in enumerate(("s5_c_re", "s5_c_im")):
                for g2_ in range(2):
                    S.dma("sp", Cn[:, :, g2_, :], c.din[nm][l, x].rearrange("(q g) c p -> q g c p", g=2)[:, g2_], writes=[Cn_b])
                for cc in range(16):
                    S.op("pe", lambda e: e.transpose(out=cp[:, cc, :], in_=Cn[:, cc, :, :].rearrange("q g p -> q (g p)"),
                                                     identity=K.identf[0:32, 0:32]), reads=[Cn_b, K.b], writes=[cp_b])
                S.op("dve", lambda e: e.tensor_copy(out=Cq[:, x, ri].rearrange("p q c -> p c q"), in_=cp[:]), reads=[cp_b], writes=[cb])
        S.barrier()
        pre.close()
        AB = sb("AB", [128, 2, 8, 2, 4, 16], F32)
        WY = sb("WY", [128, 2, 8, 2, 4, 16], F32)
        tq = sb("tq", [128, 4, 4, 16], F32)
        E = sb("E", [128, 2, 8, 2, 128], BF16)
        Cx = sb("Cx", [128, 2, 2, 4, 32], BF16)
        W1 = sb("W1", [128, 2, 8, 2, 128], BF16)
        Wp = sb("Wp", [128, 2, 8, 2, 4, 32], BF16)
        Mk = sb("Mk", [128, 2, 8, 128], BF16)
        wb = Buf()
        E3 = sb("E3", [128, 2, 8, 2, 64], BF16)
        Wp3 = sb("Wp3", [128, 2, 8, 2, 64], BF16)
        W1m = [sb("W1m%d" % i, [128, 2, 8, 2, 128], BF16) for i in range(2)]
        S.op("pool", lambda e: e.memset(E3[:], 0.0), writes=[wb])
        S.op("pool", lambda e: e.memset(Wp3[:], 0.0), writes=[wb])
        ur = sb("ur", [128, TM], BF16)
        ur_b = Buf()
        ud = sb("ud", [128, 8, N2], BF16)
        ud_b = Buf()
        xc = [sb("xc%d" % i, [128, 2, N2], F32) for i in range(1)]
        xc_b = bufs(1)
        kst = [sb("kst%d" % i, [128, 2, N2], F32) for i in range(2)]
        kst_b = bufs(2)
        Xs = sb("Xs", [128, 4, 2, 2, NC], BF16)
        Xs_b = Buf()
        yb = sb("yb", [128, TM], BF16)
        yb_b = Buf()
        GW = min(TM, 1024)
        g1 = sb("g1", [128, GW], F32)
        g1_b = Buf()
        g2 = sb("g2", [128, GW], F32)
        g2_b = Buf()
        go = [sb("go%d" % i, [128, GW], BF16) for i in range(2)]
        go_b = bufs(2)
        kp_ = ps("kp", [128, 16, 32], F32)
        kp_b = Buf(True)
        tp = ps("tp", [128, 8, 128], BF16)
        tp_b = Buf(True)
        xp = [ps("xp%d" % i, [128, 512], F32) for i in range(2)]
        xp_b = pbufs(2)
        yp = [ps("yp%d" % i, [128, 512], F32) for i in range(2)]
        yp_b = pbufs(2)
        CW = 512
        for ct in range(8):
            qs_ = slice(4 * ct, 4 * ct + 4)
            for hf in range(2):
                S.dma("sp", ur[:], dr["s5uT"][:, ct, hf * TM:(hf + 1) * TM], writes=[ur_b])
                S.op("pool", lambda e: e.tensor_copy(out=ud[:, :, hf * NC:(hf + 1) * NC], in_=ur[:].rearrange("p (n t) -> p t n", t=8)),
                     reads=[ur_b], writes=[ud_b])
            for x in range(2):
                for k in range(8):
                    def zb(ri, kk):
                        return Z[:, x, ri, kk, qs_].unsqueeze(2).to_broadcast([128, 4, 16])
                    S.op("dve", lambda e: e.tensor_tensor(out=tq[:, 0], in0=Bb[:, x, 0, qs_, :], in1=zb(0, k), op=ALU.mult), reads=[cb], writes=[wb])
                    S.op("dve", lambda e: e.tensor_tensor(out=tq[:, 1], in0=Bb[:, x, 1, qs_, :], in1=zb(1, k), op=ALU.mult), reads=[cb], writes=[wb])
                    S.op("pool", lambda e: e.tensor_tensor(out=tq[:, 2], in0=Bb[:, x, 1, qs_, :], in1=zb(0, k), op=ALU.mult), reads=[cb], writes=[wb])
                    S.op("pool", lambda e: e.tensor_tensor(out=tq[:, 3], in0=Bb[:, x, 0, qs_, :], in1=zb(1, k), op=ALU.mult), reads=[cb], writes=[wb])
                    S.op("dve", lambda e: e.tensor_tensor(out=AB[:, x, k, 0], in0=tq[:, 0], in1=tq[:, 1], op=ALU.subtract), reads=[wb], writes=[wb])
                    S.op("dve", lambda e: e.tensor_tensor(out=AB[:, x, k, 1], in0=tq[:, 2], in1=tq[:, 3], op=ALU.add), reads=[wb], writes=[wb])
                    m = k + 1
                    S.op("dve", lambda e: e.tensor_tensor(out=tq[:, 0], in0=Cq[:, x, 0, qs_, :], in1=zb(0, m), op=ALU.mult), reads=[cb, wb], writes=[wb])
                    S.op("dve", lambda e: e.tensor_tensor(out=tq[:, 1], in0=Cq[:, x, 1, qs_, :], in1=zb(1, m), op=ALU.mult), reads=[cb, wb], writes=[wb])
                    S.op("pool", lambda e: e.tensor_tensor(out=tq[:, 2], in0=Cq[:, x, 0, qs_, :], in1=zb(1, m), op=ALU.mult), reads=[cb, wb], writes=[wb])
                    S.op("pool", lambda e: e.tensor_tensor(out=tq[:, 3], in0=Cq[:, x, 1, qs_, :], in1=zb(0, m), op=ALU.mult), reads=[cb, wb], writes=[wb])
                    S.op("dve", lambda e: e.tensor_tensor(out=WY[:, x, k, 0], in0=tq[:, 0], in1=tq[:, 1], op=ALU.subtract), reads=[wb], writes=[wb])
                    S.op("dve", lambda e: e.scalar_tensor_tensor(out=WY[:, x, k, 1], in0=tq[:, 2], scalar=-1.0, in1=tq[:, 3],
                                                                 op0=ALU.mult, op1=ALU.subtract), reads=[wb], writes=[wb])
            mE = mk[:, 0, :].rearrange("p (a b c) -> p a b c", a=4, b=2)
            for x in range(2):
                for ri in range(2):
                    S.op("dve", lambda e: e.tensor_tensor(
                        out=Cx[:, x, ri].rearrange("p a (b c) -> p a b c", b=2),
                        in0=Cq[:, x, ri, qs_, :].unsqueeze(2).to_broadcast([128, 4, 2, 16]), in1=mE, op=ALU.mult), reads=[cb], writes=[wb])
                    if ri == 1:
                        S.op("dve", lambda e: e.tensor_scalar(out=Cx[:, x, 1], in0=Cx[:, x, 1], scalar1=-1.0, scalar2=None, op0=ALU.mult),
                             reads=[wb], writes=[wb])
                    for k in range(8):
                        S.op("dve", lambda e: e.tensor_tensor(
                            out=E[:, x, k, ri, :].rearrange("p (a b c) -> p a b c", a=4, b=2),
                            in0=AB[:, x, k, ri].unsqueeze(2).to_broadcast([128, 4, 2, 16]), in1=mE, op=ALU.mult), reads=[wb], writes=[wb])
            S.op("pool", lambda e: e.tensor_copy(out=E3[:, :, :, :, 32:64], in_=E[:, :, :, :, 96:128]), reads=[wb], writes=[wb])
            for x in range(2):
                for k in range(8):
                    for ri in range(2):
                        S.op("pe", lambda e: e.matmul(out=kp_[64:128, x * 8 + k, :], lhsT=E3[:, x, k, ri, :], rhs=Cx[:, x, ri, 3, :],
                                                      start=(ri == 0), stop=False, skip_group_check=True), reads=[wb], writes=[kp_b])
                    for ri in range(2):
                        S.op("pe", lambda e: e.matmul(out=kp_[64:96, x * 8 + k, :], lhsT=E[:, x, k, ri, 64:96], rhs=Cx[:, x, ri, 2, :],
                                                      start=False, stop=(ri == 1), skip_group_check=True), reads=[wb], writes=[kp_b])
                    for q4 in range(2):
                        for ri in range(2):
                            S.op("pe", lambda e: e.matmul(out=kp_[32 * q4:32 * q4 + 32, x * 8 + k, :],
                                                          lhsT=E[:, x, k, ri, 32 * q4:32 * q4 + 32], rhs=Cx[:, x, ri, q4, :],
                                                          start=(ri == 0), stop=(ri == 1), skip_group_check=True), reads=[wb], writes=[kp_b])
            for x in range(2):
                for k in range(8):
                    S.op("dve", lambda e: e.tensor_tensor(out=Mk[:, x, k, :].rearrange("p (a b) -> p a b", a=4),
                                                          in0=kp_[:, x * 8 + k, :].unsqueeze(1).to_broadcast([128, 4, 32]),
                                                          in1=mk[:, 1:5, 0:32], op=ALU.mult), reads=[kp_b, cb], writes=[wb])
            for x in range(2):
                for ri in range(2):
                    for k in range(8):
                        S.op("pe", lambda e: e.transpose(out=tp[:, k, :], in_=E[:, x, k, ri, :], identity=K.identb[:]),
                             reads=[wb, K.b], writes=[tp_b])
                    S.op("act", lambda e: e.activation(out=W1[:, x, :, ri, :], in_=tp[:], func=AF.Copy), reads=[tp_b], writes=[wb])
            for jj in range(2):
                S.op("dve" if jj == 0 else "pool", lambda e: e.tensor_scalar(
                    out=W1m[jj][64:128].rearrange("p a b c d -> p (a b c d)"), in0=W1[64:128].rearrange("p a b c d -> p (a b c d)"),
                    scalar1=mk[64:128, 3 + jj, 0:1], scalar2=None, op0=ALU.mult), reads=[wb, cb], writes=[wb])
            for x in range(2):
                for k in range(8):
                    for ri in range(2):
                        S.op("dve" if ri == 0 else "pool", lambda e: e.tensor_tensor(
                            out=Wp[:, x, k, ri].rearrange("p a (b c) -> p a b c", b=2),
                            in0=WY[:, x, k, ri].unsqueeze(2).to_broadcast([128, 4, 2, 16]), in1=mE, op=ALU.mult), reads=[wb, cb], writes=[wb])
            S.op("pool", lambda e: e.tensor_copy(out=Wp3[:, :, :, :, 32:64], in_=Wp[:, :, :, :, 3, :]), reads=[wb], writes=[wb])
            for q4 in range(4):
                q = 4 * ct + q4
                for x in range(2):
                    s = 0
                    N = NC if x == 0 else N2
                    for ri in range(2):
                        for c0 in range(0, N, CW):
                            cw = min(CW, N - c0)
                            a = (ri + c0 // CW) % 2
                            for tau in range(8):
                                k = (7 - tau) if x == 0 else tau
                                S.op("pe", lambda e: e.matmul(out=xp[a][:, 0:cw],
                                                              lhsT=(W1[32 * q4:32 * q4 + 32, x, k, ri, :] if q4 < 2 else W1m[q4 - 2][64:128, x, k, ri, :]),
                                                              rhs=(ud[32 * q4:32 * q4 + 32, tau, c0:c0 + cw] if q4 < 2 else ud[64:128, tau, c0:c0 + cw]),
                                                              start=(tau == 0), stop=(tau == 7)), reads=[wb, ud_b], writes=[xp_b[a]])
                            S.op("act", lambda e: e.activation(out=xc[s][:, ri, c0:c0 + cw], in_=xp[a][:, 0:cw], func=AF.Copy),
                                 reads=[xp_b[a]], writes=[xc_b[s]])
                    ksA = kst[0]
                    ksB = kst[1]
                    src, src_b = xc[s], xc_b[s]
                    dst, dst_b = ksA, kst_b[0]
                    j = 0
                    sh = 1
                    while sh < N:
                        ar_ = Z[:, x, 0, 8 + j, q:q + 1]
                        ai_ = Z[:, x, 1, 8 + j, q:q + 1]
                        nai = ZN[:, x, j, q:q + 1]
                        if x == 0:
                            lo, hi = slice(sh, N), slice(0, N - sh)
                            keep = slice(0, sh)
                        else:
                            lo, hi = slice(0, N - sh), slice(sh, N)
                            keep = slice(N - sh, N)
                        S.op("pool", lambda e: e.tensor_copy(out=dst[:, :, keep], in_=src[:, :, keep]), reads=[src_b], writes=[dst_b])
                        S.op("dve", lambda e: e.scalar_tensor_tensor(out=dst[:, 0, lo], in0=src[:, 0, hi], scalar=ar_, in1=src[:, 0, lo],
                                                                     op0=ALU.mult, op1=ALU.add), reads=[src_b, cb], writes=[dst_b])
                        S.op("dve", lambda e: e.scalar_tensor_tensor(out=dst[:, 0, lo], in0=src[:, 1, hi], scalar=nai, in1=dst[:, 0, lo],
                                                                     op0=ALU.mult, op1=ALU.add), reads=[src_b, cb], writes=[dst_b])
                        S.op("dve", lambda e: e.scalar_tensor_tensor(out=dst[:, 1, lo], in0=src[:, 1, hi], scalar=ar_, in1=src[:, 1, lo],
                                                                     op0=ALU.mult, op1=ALU.add), reads=[src_b, cb], writes=[dst_b])
                        S.op("dve", lambda e: e.scalar_tensor_tensor(out=dst[:, 1, lo], in0=src[:, 0, hi], scalar=ai_, in1=dst[:, 1, lo],
                                                                     op0=ALU.mult, op1=ALU.add), reads=[src_b, cb], writes=[dst_b])
                        if dst is ksA:
                            src, src_b, dst, dst_b = ksA, kst_b[0], ksB, kst_b[1]
                        else:
                            src, src_b, dst, dst_b = dst, dst_b, ksA, kst_b[0]
                        j += 1
                        sh *= 2
                    if x == 0:
                        S.op("pool", lambda e: e.memset(Xs[:, q4, 0, :, 0:1], 0.0), writes=[Xs_b])
                        S.op("act", lambda e: e.activation(out=Xs[:, q4, 0, :, 1:NC], in_=src[:, :, 0:NC - 1], func=AF.Copy),
                             reads=[src_b], writes=[Xs_b])
                    else:
                        S.op("act", lambda e: e.activation(out=Xs[:, q4, 1, :, :], in_=src[:, :, 1:NC + 1], func=AF.Copy),
                             reads=[src_b], writes=[Xs_b])
            ybv = yb[:].rearrange("p (n t) -> p t n", t=8)
            for tau in range(8):
                for c0 in range(0, NC, CW):
                    cw = min(CW, NC - c0)
                    a = (tau + c0 // CW) % 2
                    terms = []
                    for s_ in range(8):
                        if s_ <= tau:
                            terms.append((0, tau - s_, s_))
                        if s_ >= tau:
                            terms.append((1, s_ - tau, s_))
                    def intra(i):
                        x, k, s_ = terms[i]
                        S.op("pe", lambda e: e.matmul(out=yp[a][:, 0:cw], lhsT=Mk[:, x, k, :], rhs=ud[:, s_, c0:c0 + cw],
                                                      start=(i == 0), stop=(i == len(terms) - 1), skip_group_check=True),
                             reads=[wb, ud_b], writes=[yp_b[a]])
                    intra(0)
                    for q4 in range(4):
                        for x in range(2):
                            m1 = tau if x == 0 else 7 - tau
                            for ri in range(2):
                                S.op("pe", lambda e: e.matmul(out=(yp[a][32 * q4:32 * q4 + 32, 0:cw] if q4 < 3 else yp[a][64:128, 0:cw]),
                                                              lhsT=(Wp[:, x, m1, ri, q4, :] if q4 < 3 else Wp3[:, x, m1, ri, :]),
                                                              rhs=Xs[:, q4, x, ri, c0:c0 + cw], start=False, stop=False,
                                                              skip_group_check=True),
                                     reads=[wb, Xs_b], writes=[yp_b[a]])
                    for i in range(1, len(terms)):
                        intra(i)
                    S.op("dve", lambda e: e.scalar_tensor_tensor(out=ybv[:, tau, c0:c0 + cw], in0=ud[:, tau, c0:c0 + cw],
                                                                 scalar=vec[:, V_S5D + ct:V_S5D + ct + 1], in1=yp[a][:, 0:cw],
                                                                 op0=ALU.mult, op1=ALU.add), reads=[yp_b[a], ud_b, cb], writes=[yb_b])
            for gi in range(TM // GW):
                gs = slice(gi * GW, (gi + 1) * GW)
                o = gi % 2
                S.op("act", lambda e: e.activation(out=g1[:], in_=yb[:, gs], func=AF.Square), reads=[yb_b], writes=[g1_b])
                S.op("dve", lambda e: e.tensor_scalar(out=g1[:], in0=g1[:], scalar1=0.044715, scalar2=1.0, op0=ALU.mult, op1=ALU.add),
                     reads=[g1_b], writes=[g1_b])
                S.op("pool", lambda e: e.tensor_tensor(out=g2[:], in0=g1[:], in1=yb[:, gs], op=ALU.mult), reads=[g1_b, yb_b], writes=[g2_b])
                S.op("act", lambda e: e.activation(out=g2[:], in_=g2[:], func=AF.Sigmoid, scale=2.0 * math.sqrt(2.0 / math.pi)),
                     reads=[g2_b], writes=[g2_b])
                S.op("dve", lambda e: e.tensor_tensor(out=go[o][:], in0=g2[:], in1=yb[:, gs], op=ALU.mult), reads=[g2_b, yb_b], writes=[go_b[o]])
                S.dma("sp", dr["ys5T"][:, ct, gs], go[o][:], reads=[go_b[o]])
        S.barrier()
    NT = TM // TT
    with ExitStack() as ph:
        sb, ps = alloc_helpers(nc, ph)
        wgl = sb("wgl", [128, 8, 2 * D], BF16)
        wgl_b = Buf()
        load_w_cast(S, wgl, wgl_b, c.din["s5_w_glu"][l], 8, 2 * D)
        gt = [sb("gt%d" % i, [128, 8, TT], BF16) for i in range(2)]
        gt_b = bufs(2)
        sg = [sb("sg%d" % i, [128, TT], F32) for i in range(2)]
        sg_b = bufs(2)
        ob = [sb("ob%d" % i, [128, 8, TT], BF16) for i in range(2)]
        ob_b = bufs(2)
        pa = [ps("pa%d" % i, [128, TT], F32) for i in range(3)]
        pa_b = pbufs(3)
        pb = [ps("pb%d" % i, [128, TT], F32) for i in range(3)]
        pb_b = pbufs(3)
        it = 0
        for j in range(NT):
            s = j % 2
            tsl = slice(j * TT, (j + 1) * TT)
            S.dma("sp", gt[s][:], dr["ys5T"][:, :, tsl], writes=[gt_b[s]])
            for m in range(8):
                a = it % 3
                t = it % 2
                it += 1
                for kc in range(8):
                    S.op("pe", lambda e: e.matmul(out=pa[a][:], lhsT=wgl[:, kc, m * 128:(m + 1) * 128], rhs=gt[s][:, kc, :],
                                                  start=(kc == 0), stop=(kc == 7)), reads=[wgl_b, gt_b[s]], writes=[pa_b[a]])
                for kc in range(8):
                    S.op("pe", lambda e: e.matmul(out=pb[a][:], lhsT=wgl[:, kc, D + m * 128:D + (m + 1) * 128], rhs=gt[s][:, kc, :],
                                                  start=(kc == 0), stop=(kc == 7)), reads=[wgl_b, gt_b[s]], writes=[pb_b[a]])
                S.op("act", lambda e: e.activation(out=sg[t][:], in_=pb[a][:], func=AF.Sigmoid), reads=[pb_b[a]], writes=[sg_b[t]])
                S.op("dve", lambda e: e.tensor_tensor(out=ob[s][:, m, :], in0=pa[a][:], in1=sg[t][:], op=ALU.mult),
                     reads=[pa_b[a], sg_b[t]], writes=[ob_b[s]])
            S.dma("sp", dr["gs5T"][:, :, tsl], ob[s][:], reads=[ob_b[s]])
        S.barrier()


_PROG = {}


def _program(TM, final):
    key = (TM, final)
    if key not in _PROG:
        _PROG[key] = build(TM, 1, "AMBRSE", final=final)[0]
    return _PROG[key]


def kernel(**inputs):
    inp = {k: np.asarray(v) for k, v in inputs.items()}
    x = np.ascontiguousarray(inp["x"], dtype=np.float32)
    B, S_len, _ = x.shape
    TM = S_len // 2
    depth = inp["w_in"].shape[0]
    for l in range(depth):
        nc = _program(TM, l == depth - 1)
        maps = core_maps(x, inp, [l], TM)
        res = run_bass_kernel_spmd(nc, maps, core_ids=list(range(2 * B)))
        xn = np.empty_like(x)
        for b in range(B):
            xn[b, :TM] = np.asarray(res.results[2 * b]["out"], np.float32)
            xn[b, TM:] = np.asarray(res.results[2 * b + 1]["out"], np.float32)[::-1]
        x = xn
    return x
```
